# Optimizing a Trainium2 kernel written in Bass

```python
import math
import jax, jax.numpy as jnp
from jax import lax
import numpy as np

D_MODEL = 1024
BATCH = 8
SEQ = 4096
DEPTH = 1

HEAD_DIM = 64
A_HEADS = 8
A_WIDTH = A_HEADS * HEAD_DIM
CHUNK = 128
B_HEADS = 8
B_KV_GROUPS = 2
B_HPG = B_HEADS // B_KV_GROUPS
B_WIDTH = B_HEADS * HEAD_DIM
KV_WIDTH = B_KV_GROUPS * HEAD_DIM
CMP_LEN = 32
CMP_STRIDE = 16
CMP_HIDDEN = 256
SEL_BLOCK = 64
SEL_TOPK = 16
WINDOW = 512
Q_BLOCK = 128
N_BUCKETS = 32
MAX_DISTANCE = 128
D_FF = 2816
MIX_WIDTH = A_WIDTH + B_WIDTH
SPLIT_WIDTHS = (2 * A_WIDTH, B_WIDTH, KV_WIDTH, KV_WIDTH, KV_WIDTH, KV_WIDTH, KV_WIDTH, KV_WIDTH, 3 * B_HEADS)
IN_COLS = sum(SPLIT_WIDTHS)
SPLIT_OFFSETS = tuple(int(v) for v in np.cumsum(SPLIT_WIDTHS)[:-1])
EPS = 1e-6
NEG = -1e30
FORCE = 1e6
SCALE = HEAD_DIM ** -0.5

kernel_name = "hymba_gmlp_nsa_macaron_layer"


def rms_norm(x, g):
    xf = x.astype(jnp.float32)
    y = xf * lax.rsqrt(jnp.mean(xf * xf, axis=-1, keepdims=True) + EPS)
    return (y * g.astype(jnp.float32)).astype(x.dtype)


def swiglu(x, w_gate, w_up, w_down):
    return (jax.nn.silu(x @ w_gate) * (x @ w_up)) @ w_down


def t5_bucket(dist):
    n = jnp.maximum(dist, 0)
    max_exact = N_BUCKETS // 2
    nf = jnp.maximum(n, 1).astype(jnp.float32)
    large = max_exact + (jnp.log(nf / max_exact) / math.log(MAX_DISTANCE / max_exact)
                         * (N_BUCKETS - max_exact)).astype(jnp.int32)
    large = jnp.minimum(large, N_BUCKETS - 1)
    return jnp.where(n < max_exact, n, large)


def gmlp_mixer(z, v_norm_g, w_s, b_s):
    u, v = jnp.split(z, 2, axis=-1)
    v = rms_norm(v, v_norm_g)
    bsz, t, _ = v.shape
    vc = v.reshape(bsz, t // CHUNK, CHUNK, A_HEADS, HEAD_DIM)
    mask = jnp.tril(jnp.ones((CHUNK, CHUNK), dtype=bool))
    w = jnp.where(mask, w_s, 0.0)
    sv = jnp.einsum('hts,bcshd->bcthd', w, vc) + b_s.T[None, None, :, :, None]
    return u * sv.reshape(bsz, t, A_WIDTH)


def compress(kv, pe, w1, b1, w2, b2):
    bsz, g, t, dh = kv.shape
    r = CMP_LEN // CMP_STRIDE
    seg = kv.reshape(bsz, g, t // CMP_STRIDE, CMP_STRIDE, dh)
    n_cmp = t // CMP_STRIDE - r + 1
    blocks = jnp.concatenate([seg[:, :, i:i + n_cmp] for i in range(r)], axis=3)
    blocks = (blocks + pe).reshape(bsz, g, n_cmp, CMP_LEN * dh)
    return jax.nn.gelu(blocks @ w1 + b1) @ w2 + b2


def nsa_mixer(q, k_c, v_c, k_s, v_s, k_w, v_w, gate_logits, q_norm_g, k_norm_g,
              cmp_pe, cmp_w1, cmp_b1, cmp_w2, cmp_b2, rel_bias):
    bsz, t, _ = q.shape
    G, Hg, dh = B_KV_GROUPS, B_HPG, HEAD_DIM
    q = rms_norm(q.reshape(bsz, t, G, Hg, dh), q_norm_g).transpose(0, 2, 3, 1, 4)

    def kv_heads(a):
        return a.reshape(bsz, t, G, dh).transpose(0, 2, 1, 3)

    kc = rms_norm(compress(kv_heads(k_c), cmp_pe[0], cmp_w1[0], cmp_b1[0], cmp_w2[0], cmp_b2[0]), k_norm_g[0])
    vc = compress(kv_heads(v_c), cmp_pe[1], cmp_w1[1], cmp_b1[1], cmp_w2[1], cmp_b2[1])
    nsb = t // SEL_BLOCK
    ks_blocks = rms_norm(kv_heads(k_s), k_norm_g[1]).reshape(bsz, G, nsb, SEL_BLOCK, dh)
    vs_blocks = kv_heads(v_s).reshape(bsz, G, nsb, SEL_BLOCK, dh)
    pad = ((0, 0), (0, 0), (WINDOW, 0), (0, 0))
    kw_pad = jnp.pad(rms_norm(kv_heads(k_w), k_norm_g[2]), pad)
    vw_pad = jnp.pad(kv_heads(v_w), pad)
    gates = jax.nn.sigmoid(gate_logits.reshape(bsz, t, 3, G, Hg)).transpose(0, 3, 4, 1, 2)
    rb = rel_bias.reshape(G, Hg, N_BUCKETS)

    n_cmp = kc.shape[2]
    cmp_end = jnp.arange(n_cmp) * CMP_STRIDE + CMP_LEN - 1
    sel_start = jnp.arange(nsb) * SEL_BLOCK
    overlap = (((cmp_end[:, None] - (CMP_LEN - 1)) <= (sel_start[None, :] + SEL_BLOCK - 1))
               & (cmp_end[:, None] >= sel_start[None, :])).astype(jnp.float32)
    n_top = min(SEL_TOPK, nsb)
    bi = jnp.arange(bsz)[:, None, None, None]
    gi = jnp.arange(G)[None, :, None, None]
    g6 = jnp.arange(G)[None, :, None, None, None, None]
    h6 = jnp.arange(Hg)[None, None, :, None, None, None]
    blk = jnp.arange(nsb)

    def attend_block(c):
        t0 = c * Q_BLOCK
        tpos = t0 + jnp.arange(Q_BLOCK)
        qb = lax.dynamic_slice_in_dim(q, t0, Q_BLOCK, axis=3)
        dist_c = tpos[:, None] - cmp_end[None, :]
        valid_c = dist_c >= 0
        s_c = (jnp.einsum('bghqd,bgnd->bghqn', qb, kc, preferred_element_type=jnp.float32) * SCALE
               + rb[:, :, t5_bucket(dist_c)])
        p_c = jax.nn.softmax(jnp.where(valid_c, s_c, NEG), axis=-1) * valid_c
        o_c = jnp.einsum('bghqn,bgnd->bghqd', p_c, vc)
        imp = jnp.einsum('bghqn,nj->bgqj', p_c, overlap)
        cur = (tpos // SEL_BLOCK)[:, None]
        forced = (blk == 0) | (blk == cur) | (blk == cur - 1)
        eligible = sel_start[None, :] <= tpos[:, None]
        imp = jnp.where(eligible, jnp.where(forced, FORCE, imp), -1.0)
        _, idx = lax.top_k(imp, n_top)
        kb = ks_blocks[bi, gi, idx]
        vb = vs_blocks[bi, gi, idx]
        dist_s = tpos[:, None, None] - (idx[..., None] * SEL_BLOCK + jnp.arange(SEL_BLOCK))
        valid_s = (dist_s >= 0)[:, :, None]
        s_s = (jnp.einsum('bghqd,bgqksd->bghqks', qb, kb, preferred_element_type=jnp.float32) * SCALE
               + rb[g6, h6, t5_bucket(dist_s)[:, :, None]])
        s_s = jnp.where(valid_s, s_s, NEG)
        p_s = jax.nn.softmax(s_s.reshape(bsz, G, Hg, Q_BLOCK, -1), axis=-1).reshape(s_s.shape)
        o_s = jnp.einsum('bghqks,bgqksd->bghqd', p_s, vb)
        kwb = lax.dynamic_slice_in_dim(kw_pad, t0, WINDOW + Q_BLOCK, axis=2)
        vwb = lax.dynamic_slice_in_dim(vw_pad, t0, WINDOW + Q_BLOCK, axis=2)
        kpos = t0 - WINDOW + jnp.arange(WINDOW + Q_BLOCK)
        dist_w = tpos[:, None] - kpos[None, :]
        valid_w = (dist_w >= 0) & (dist_w < WINDOW) & (kpos[None, :] >= 0)
        s_w = (jnp.einsum('bghqd,bgkd->bghqk', qb, kwb, preferred_element_type=jnp.float32) * SCALE
               + rb[:, :, t5_bucket(dist_w)])
        p_w = jax.nn.softmax(jnp.where(valid_w, s_w, NEG), axis=-1)
        o_w = jnp.einsum('bghqk,bgkd->bghqd', p_w, vwb)
        gb = lax.dynamic_slice_in_dim(gates, t0, Q_BLOCK, axis=3)
        o = gb[..., 0:1] * o_c + gb[..., 1:2] * o_s + gb[..., 2:3] * o_w
        return o.astype(q.dtype)

    out = lax.map(attend_block, jnp.arange(t // Q_BLOCK))
    return out.transpose(1, 0, 4, 2, 3, 5).reshape(bsz, t, B_WIDTH)


def setup_inputs(seed: int = 0) -> dict:
    key = jax.random.key(seed)
    ks = jax.random.split(key, 32)
    nrm = lambda k, shape, s: jax.random.normal(k, shape, jnp.float32) * s
    gain = lambda k, shape: 1.0 + 0.05 * jax.random.normal(k, shape, jnp.float32)
    L = DEPTH
    return {
        "x": nrm(ks[0], (BATCH, SEQ, D_MODEL), 1.0),
        "ffn1_norm_g": gain(ks[1], (L, D_MODEL)),
        "ffn1_w_gate": nrm(ks[2], (L, D_MODEL, D_FF), D_MODEL ** -0.5),
        "ffn1_w_up": nrm(ks[3], (L, D_MODEL, D_FF), D_MODEL ** -0.5),
        "ffn1_w_down": nrm(ks[4], (L, D_FF, D_MODEL), D_FF ** -0.5),
        "mix_norm_g": gain(ks[5], (L, D_MODEL)),
        "w_in": nrm(ks[6], (L, D_MODEL, IN_COLS), D_MODEL ** -0.5),
        "gmlp_v_norm_g": gain(ks[7], (L, A_WIDTH)),
        "gmlp_w_s": nrm(ks[8], (L, A_HEADS, CHUNK, CHUNK), 0.5 * CHUNK ** -0.5),
        "gmlp_b_s": gain(ks[9], (L, A_HEADS, CHUNK)),
        "q_norm_g": gain(ks[10], (L, HEAD_DIM)),
        "k_norm_g": gain(ks[11], (L, 3, HEAD_DIM)),
        "cmp_pe": nrm(ks[12], (L, 2, CMP_LEN, HEAD_DIM), 0.1),
        "cmp_w1": nrm(ks[13], (L, 2, CMP_LEN * HEAD_DIM, CMP_HIDDEN), (CMP_LEN * HEAD_DIM) ** -0.5),
        "cmp_b1": nrm(ks[14], (L, 2, CMP_HIDDEN), 0.02),
        "cmp_w2": nrm(ks[15], (L, 2, CMP_HIDDEN, HEAD_DIM), CMP_HIDDEN ** -0.5),
        "cmp_b2": nrm(ks[16], (L, 2, HEAD_DIM), 0.02),
        "rel_bias": nrm(ks[17], (B_HEADS, N_BUCKETS), 0.5),
        "w_out": nrm(ks[18], (L, MIX_WIDTH, D_MODEL), MIX_WIDTH ** -0.5),
        "ffn2_norm_g": gain(ks[19], (L, D_MODEL)),
        "ffn2_w_gate": nrm(ks[20], (L, D_MODEL, D_FF), D_MODEL ** -0.5),
        "ffn2_w_up": nrm(ks[21], (L, D_MODEL, D_FF), D_MODEL ** -0.5),
        "ffn2_w_down": nrm(ks[22], (L, D_FF, D_MODEL), D_FF ** -0.5),
        "final_norm_g": gain(ks[23], (L, D_MODEL)),
    }


def reference(x, ffn1_norm_g, ffn1_w_gate, ffn1_w_up, ffn1_w_down, mix_norm_g, w_in,
              gmlp_v_norm_g, gmlp_w_s, gmlp_b_s, q_norm_g, k_norm_g, cmp_pe, cmp_w1, cmp_b1,
              cmp_w2, cmp_b2, rel_bias, w_out, ffn2_norm_g, ffn2_w_gate, ffn2_w_up, ffn2_w_down,
              final_norm_g):
    for l in range(DEPTH):
        h = rms_norm(x, ffn1_norm_g[l])
        x = x + 0.5 * swiglu(h, ffn1_w_gate[l], ffn1_w_up[l], ffn1_w_down[l])
        h = rms_norm(x, mix_norm_g[l])
        proj = h @ w_in[l]
        z_a, q, k_c, v_c, k_s, v_s, k_w, v_w, g = jnp.split(proj, SPLIT_OFFSETS, axis=-1)
        y_a = gmlp_mixer(jax.nn.gelu(z_a), gmlp_v_norm_g[l], gmlp_w_s[l], gmlp_b_s[l])
        y_b = nsa_mixer(q, k_c, v_c, k_s, v_s, k_w, v_w, g, q_norm_g[l], k_norm_g[l],
                        cmp_pe[l], cmp_w1[l], cmp_b1[l], cmp_w2[l], cmp_b2[l], rel_bias)
        x = x + jnp.concatenate([y_a, y_b], axis=-1) @ w_out[l]
        h = rms_norm(x, ffn2_norm_g[l])
        x = x + 0.5 * swiglu(h, ffn2_w_gate[l], ffn2_w_up[l], ffn2_w_down[l])
        x = rms_norm(x, final_norm_g[l])
    return x
```

```python
import math
from contextlib import ExitStack
import numpy as np
import concourse.bass as bass
import concourse.mybir as mybir
from concourse.bass_utils import run_bass_kernel_spmd

F32 = mybir.dt.float32
BF16 = mybir.dt.bfloat16
AF = mybir.ActivationFunctionType
ALU = mybir.AluOpType

ENGS = ("pe", "act", "dve", "pool", "sp")
T = 4096
D = 1024
DFF = 2816
NF = 22
TB = 512
NBLK = T // TB
EPS = 1e-6
BIG = 30000.0
DEBUG = False


class Prog:
    N_DMA_SEMS = 12

    def __init__(self, nc):
        self.nc = nc
        self.ops = []
        self.last_write = {}
        self.readers = {}

    def op(self, eng, fn, reads=(), writes=(), dma=False):
        i = len(self.ops)
        deps = set()
        for r in reads:
            w = self.last_write.get(r)
            if w is not None:
                deps.add(w)
        for w_ in writes:
            w = self.last_write.get(w_)
            if w is not None:
                deps.add(w)
            for rd in self.readers.get(w_, ()):
                deps.add(rd)
        deps.discard(i)
        self.ops.append(dict(eng=eng, fn=fn, deps=deps, dma=dma, marked=dma))
        for r in reads:
            self.readers.setdefault(r, []).append(i)
        for w_ in writes:
            self.last_write[w_] = i
            self.readers[w_] = []
        return i

    def dma(self, eng, out, in_, reads=(), writes=(), **kw):
        def fn(e, out=out, in_=in_, kw=kw):
            return e.dma_start(out=out, in_=in_, **kw)
        return self.op(eng, fn, reads, writes, dma=True)

    def emit(self, final_wait_eng="sp"):
        nc = self.nc
        ops = self.ops
        for i, o in enumerate(ops):
            latest = {}
            dd = set()
            for d in o["deps"]:
                od = ops[d]
                if od["dma"]:
                    dd.add(d)
                    continue
                if o["eng"] == "pe" and od["eng"] == "pe":
                    continue
                if latest.get(od["eng"], -1) < d:
                    latest[od["eng"]] = d
            o["deps"] = dd | set(latest.values())
        for o in ops:
            for d in o["deps"]:
                ops[d]["marked"] = True
        with ExitStack() as es:
            esem = {e: es.enter_context(nc.semaphore("s_" + e)) for e in ENGS}
            dsems = {e: [es.enter_context(nc.semaphore("d_%s%d" % (e, k))) for k in range(self.N_DMA_SEMS)]
                     for e in ("sp", "pool")}
            cnt = {e: 0 for e in ENGS}
            dcnt = {e: [0] * self.N_DMA_SEMS for e in dsems}
            drr = {e: 0 for e in dsems}
            for o in ops:
                if o["dma"]:
                    e = o["eng"]
                    k = drr[e]
                    drr[e] = (k + 1) % self.N_DMA_SEMS
                    o["prev"] = dcnt[e][k]
                    dcnt[e][k] += 16
                    o["sem"] = dsems[e][k]
                    o["val"] = dcnt[e][k]
                elif o["marked"]:
                    cnt[o["eng"]] += 1
                    o["sem"] = esem[o["eng"]]
                    o["val"] = cnt[o["eng"]]
            by_eng = {e: [o for o in ops if o["eng"] == e] for e in ENGS}
            all_dma = [o for o in ops if o["dma"]]
            self.stats = {e: len(by_eng[e]) for e in ENGS}
            with nc.Block() as block:
                def run(e_name, eng):
                    known = {}

                    def wait_all(pairs):
                        best = {}
                        for sem, val in pairs:
                            if best.get(sem.num, (None, 0))[1] < val:
                                best[sem.num] = (sem, val)
                        for key, (sem, val) in best.items():
                            if known.get(key, 0) >= val:
                                continue
                            known[key] = val
                            eng.wait_ge(sem, val)
                    for o in by_eng[e_name]:
                        pairs = [(ops[d]["sem"], ops[d]["val"]) for d in o["deps"]]
                        if o["dma"] and o["prev"] > 0:
                            pairs.append((o["sem"], o["prev"]))
                        wait_all(pairs)
                        ins = o["fn"](eng)
                        if o["marked"]:
                            ins.then_inc(o["sem"], 16 if o["dma"] else 1)
                    if e_name == final_wait_eng:
                        wait_all([(o["sem"], o["val"]) for o in all_dma])

                @block.tensor
                def _(eng):
                    run("pe", eng)

                @block.scalar
                def _(eng):
                    run("act", eng)

                @block.vector
                def _(eng):
                    run("dve", eng)

                @block.gpsimd
                def _(eng):
                    run("pool", eng)

                @block.sync
                def _(eng):
                    run("sp", eng)


def _t5_bucket_np(d):
    n = np.maximum(d, 0)
    nf = np.maximum(n, 1).astype(np.float32)
    large = 16 + (np.log(nf / np.float32(16.0)) / np.float32(math.log(8.0)) * np.float32(16.0)).astype(np.int32)
    large = np.minimum(large, 31)
    return np.where(n < 16, n, large)


def _static_consts():
    c = {}
    i = np.arange(512)
    d = i - 256
    valid = d >= 0
    bk = _t5_bucket_np(d)
    oh = np.zeros((33, 512), np.float32)
    for b in range(32):
        oh[b] = ((bk == b) & valid).astype(np.float32)
    oh[31] -= valid.astype(np.float32)
    oh[32] = -BIG * (~valid).astype(np.float32)
    c["c_oh"] = oh
    m = np.zeros((32, 128, 2, 64), np.float32)
    for cc in range(32):
        for p in range(128):
            cur = (128 * cc + p) // 64
            j = np.arange(64)
            elig = j <= cur
            forced = (j == 0) | (j == cur) | (j == cur - 1)
            m[cc, p, 0] = np.where(elig, np.where(forced, 1e6, 0.0), -1e9)
    c["c_selmask"] = m[:, :, 0, 1:62].reshape(8, 4, 128, 61).transpose(0, 2, 1, 3).copy()
    p = np.arange(128)[:, None]
    j = np.arange(128)[None, :]
    c["c_tw4"] = np.where(p > j, 0.0, -8.0 * BIG).astype(np.float32)
    s = np.arange(T)
    c["c_erows"] = (s[None, :] // 64 == np.arange(64)[:, None]).astype(np.float32)
    n = np.arange(256)[:, None]
    jb = np.arange(64)[None, :]
    ov = ((16 * n <= 64 * jb + 63) & (16 * n + 31 >= 64 * jb) & (n < 255)).astype(np.float32)
    one = (np.arange(256) < 255).astype(np.float32)[:, None]
    st = np.concatenate([one, ov[:, 1:62]], axis=1)
    c["c_vcf_static"] = st.reshape(2, 128, 62).transpose(1, 0, 2).copy()
    vn = np.zeros((16, 32, 62), np.float32)
    for cc in range(32):
        for pp in range(16):
            nn = 8 * cc - 8 + pp
            if 0 <= nn <= 254:
                vn[pp, cc] = st[nn]
    c["c_vcn_static"] = vn
    c["c_ident"] = np.eye(128, dtype=np.float32)
    bd = np.zeros((128, 128), np.float32)
    bd[:64, :64] = 1.0
    bd[64:, 64:] = 1.0
    c["c_bd"] = bd
    c["c_ones"] = np.ones((128, 128), np.float32)
    c["c_tril"] = (np.arange(128)[:, None] <= np.arange(128)[None, :]).astype(np.float32)
    return c


def _prep_weights(inp):
    f = lambda a: np.ascontiguousarray(a, dtype=np.float32)
    w = {}
    for tag, pre in (("1", "ffn1"), ("2", "ffn2")):
        wg = inp[pre + "_w_gate"][0]
        wu = inp[pre + "_w_up"][0]
        wd = inp[pre + "_w_down"][0]
        w["wg" + tag] = f(wg.reshape(8, 128, NF, 128).transpose(2, 1, 0, 3))
        w["wu" + tag] = f(wu.reshape(8, 128, NF, 128).transpose(2, 1, 0, 3))
        w["wd" + tag] = f(wd.reshape(NF, 128, 8, 128).transpose(2, 1, 0, 3))
    win = inp["w_in"][0]
    cols = []
    cols += [np.arange(j * 128, (j + 1) * 128) for j in range(4)]
    for i in range(4):
        cols.append(np.concatenate([1024 + (0 * 4 + i) * 64 + np.arange(64), 1024 + (1 * 4 + i) * 64 + np.arange(64)]))
    cols.append(1536 + np.arange(128))
    cols.append(1664 + np.arange(128))
    cols.append(1792 + np.arange(128))
    cols.append(2048 + np.arange(128))
    wfm = np.stack([win[:, cidx] for cidx in cols], 0)
    w["wfm"] = f(wfm.reshape(12, 8, 128, 128).transpose(0, 2, 1, 3))
    w["wtm_v"] = f(win[:, 512:1024].reshape(8, 128, 512).transpose(1, 0, 2))
    rcols = np.concatenate([1920 + np.arange(128), 2176 + np.arange(128), 2304 + np.arange(24)])
    w["wtm_r"] = f(win[:, rcols].reshape(8, 128, 280).transpose(1, 0, 2))
    wo = inp["w_out"][0]
    w["wo"] = f(wo.reshape(8, 128, 8, 128).transpose(2, 1, 0, 3))
    w["w1"] = f(inp["cmp_w1"][0].reshape(2, 32, 64, 256).transpose(0, 2, 1, 3))
    w["w2"] = f(inp["cmp_w2"][0].reshape(2, 2, 128, 64).transpose(2, 0, 1, 3))
    w["pe"] = f(inp["cmp_pe"][0].transpose(2, 0, 1))
    w["b1"] = f(inp["cmp_b1"][0])
    w["b2"] = f(inp["cmp_b2"][0])
    for nm in ("ffn1_norm_g", "mix_norm_g", "ffn2_norm_g", "final_norm_g"):
        w[nm] = f(inp[nm][0].reshape(8, 128).T)
    w["gv"] = f(inp["gmlp_v_norm_g"])
    w["wst"] = f(inp["gmlp_w_s"][0].transpose(2, 0, 1))
    w["bs"] = f(inp["gmlp_b_s"][0])
    w["qg"] = f(inp["q_norm_g"][0].reshape(64, 1))
    w["kg"] = f(inp["k_norm_g"][0].T)
    w["rbt"] = f(inp["rel_bias"].T)
    return w


def build(shapes):
    nc = bass.Bass("TRN2", target_bir_lowering=False)
    P = Prog(nc)
    es = ExitStack()
    dr = {}
    for nm, shp in shapes.items():
        dr[nm] = nc.dram_tensor(nm, list(shp), F32, kind="ExternalInput").ap()
    outT = nc.dram_tensor("outT", [D, T], F32, kind="ExternalOutput").ap()
    fsc = nc.dram_tensor("fsc", [8, 128, 512], F32)
    dbg = {}

    def sb(name, shape, dt):
        return es.enter_context(nc.sbuf_tensor(name, shape, dt))

    xb = sb("xb", [128, 8, TB], F32)
    hT = sb("hT", [128, 8, TB], BF16)
    actT = sb("actT", [128, NF, TB], BF16)
    wA = [sb("wA%d" % i, [128, 8, 128], BF16) for i in range(6)]
    wD = [sb("wD%d" % i, [128, NF, 128], BF16) for i in range(2)]
    Fr = [sb("Fr%d" % i, [128, 512], F32) for i in range(5)]
    PT = [sb("PT%d" % i, [128, 512], BF16) for i in range(6)]
    uT = sb("uT", [128, 4, TB], BF16)
    yT = sb("yT", [128, 2, 8, TB], BF16)
    QS = sb("QS", [128, 2, 2, 4, 4, 128], BF16)
    yb = sb("yb", [128, 512], F32)
    ybb = sb("ybb", [128, 512], BF16)
    NS = [sb("NS%d" % g, [128, 128], BF16) for g in range(2)]
    KS = [sb("KS%d" % g, [128, T], BF16) for g in range(2)]
    KW = sb("KW", [128, 2048], BF16)
    Vs = sb("Vs", [128, 2, 32, 65], BF16)
    Vw = sb("Vw", [128, 2, 16, 65], BF16)
    csrc = [sb("csrc%d" % kv, [128, 16 + TB], BF16) for kv in range(2)]
    kcT = sb("kcT", [128, 264], BF16)
    hidT = sb("hidT", [128, 2, 2, 2, 272], BF16)
    VCF = sb("VCF", [128, 2, 2, 126], BF16)
    VCN = sb("VCN", [16, 2, 2, 4, 126], BF16)
    TD = [sb("TD%d" % i, [128, 2, 4, 128], BF16) for i in range(2)]
    TS1 = [sb("TS1%d" % i, [128, 2, 4, 128], BF16) for i in range(2)]
    TCn = [sb("TCn%d" % i, [16, 2, 4, 128], BF16) for i in range(2)]
    TW4 = sb("TW4", [128, 128], BF16)
    gates = sb("gates", [128, 2, 4, 24], F32)
    selm = sb("selm", [128, 4, 61], F32)
    ident = sb("ident", [128, 128], BF16)
    bdm = sb("bdm", [128, 128], BF16)
    onesm = sb("onesm", [128, 128], BF16)
    WsT = sb("WsT", [128, 8, 128], BF16)
    bT = sb("bT", [128, 4, 128], F32)
    gvrow = sb("gvrow", [128, 512], F32)
    hbias = sb("hbias", [128, 2, 256], F32)
    b2row = sb("b2row", [128, 64], F32)
    b2col = sb("b2col", [128, 1], F32)
    w2sb = sb("w2sb", [128, 2, 2, 64], BF16)
    pesb = sb("pesb", [128, 2, 32], BF16)
    gcols = {nm: sb("g_" + nm, [128, 8], F32) for nm in ("ffn1_norm_g", "mix_norm_g", "ffn2_norm_g", "final_norm_g")}
    qgc = sb("qgc", [128, 1], F32)
    kgc = sb("kgc", [128, 3], F32)
    rbx = sb("rbx", [33, 8], F32)
    small = sb("small", [128, 64], F32)
    impb = sb("impb", [128, 61], F32)
    impw = sb("impw", [128, 61], F32)
    m8a = sb("m8a", [128, 8], F32)
    m8b = sb("m8b", [128, 8], F32)
    hid32 = sb("hid32", [32, 256], F32)
    hidb = sb("hidb", [32, 256], BF16)
    kc32 = sb("kc32", [128, 32], F32)
    kcsq = sb("kcsq", [128, 32], BF16)
    kcr = sb("kcr", [128, 32], F32)
    vtmp = sb("vtmp", [128, 4, 64], F32)
    pcs = [sb("pcs%d" % i, [128, 8, 128], BF16) for i in range(2)]
    rstdb = sb("rstdb", [128, 512], F32)
    sqb = sb("sqb", [128, 512], BF16)
    ohsb = sb("ohsb", [33, 512], F32)
    vnb = sb("vnb", [128, 512], BF16)

    psb = [es.enter_context(nc.psum_tensor("psb%d" % i, [128, 512], F32)) for i in range(7)]
    ps7 = es.enter_context(nc.psum_tensor("ps7", [128, 1024], BF16))

    rr = {}

    def ring(name, n):
        i = rr.get(name, 0)
        rr[name] = (i + 1) % n
        return i

    def MM(out, lhsT, rhs, start, stop, reads, writes, **kw):
        return P.op("pe", lambda e: e.matmul(out, lhsT=lhsT, rhs=rhs, start=start, stop=stop, **kw), reads, writes)

    def TR(out, in_, reads, writes):
        return P.op("pe", lambda e: e.transpose(out, in_, ident[0:in_.shape[0], 0:in_.shape[0]]), list(reads) + ["ident"], writes)

    def ACT(out, in_, func, reads, writes, **kw):
        return P.op("act", lambda e: e.activation(out=out, in_=in_, func=func, **kw), reads, writes)

    def DV(name, reads, writes, *a, **kw):
        return P.op("dve", lambda e: getattr(e, name)(*a, **kw), reads, writes)

    def STT(out, in0, scalar, in1, op0, op1, reads, writes):
        return P.op("dve", lambda e: e.scalar_tensor_tensor(out=out, in0=in0, scalar=scalar, in1=in1, op0=op0, op1=op1), reads, writes)

    def TS(out, in0, s1, s2, op0, op1, reads, writes):
        if op1 is None:
            return P.op("dve", lambda e: e.tensor_scalar(out=out, in0=in0, scalar1=s1, scalar2=None, op0=op0), reads, writes)
        return P.op("dve", lambda e: e.tensor_scalar(out=out, in0=in0, scalar1=s1, scalar2=s2, op0=op0, op1=op1), reads, writes)

    def TT(out, in0, in1, op, reads, writes):
        return P.op("dve", lambda e: e.tensor_tensor(out=out, in0=in0, in1=in1, op=op), reads, writes)

    def bcast_rows(src2d, nparts):
        a = src2d.ap
        return bass.AP(src2d.tensor, src2d.offset, [[0, nparts], [a[-1][0], a[-1][1]]])

    P.dma("pool", ident[:], dr["c_ident"], writes=["ident"])
    P.dma("pool", bdm[:], dr["c_bd"], writes=["bdm"])
    P.dma("pool", onesm[:], dr["c_ones"], writes=["onesm"])
    for nm in gcols:
        P.dma("sp", gcols[nm][:], dr[nm], writes=[nm])
    P.dma("sp", qgc[0:64, :], dr["qg"], writes=["qgc"])
    P.dma("sp", qgc[64:128, :], dr["qg"], writes=["qgc"])
    P.dma("sp", kgc[0:64, :], dr["kg"], writes=["kgc"])
    P.dma("sp", kgc[64:128, :], dr["kg"], writes=["kgc"])
    P.dma("pool", TW4[:], dr["c_tw4"], writes=["TW4"])
    P.dma("sp", gvrow[:], bcast_rows(dr["gv"], 128), writes=["gvrow"])
    P.dma("sp", b2row[:], bcast_rows(dr["b2"][1:2, :], 128), writes=["b2row"])
    b2c = dr["b2"][0:1, :]
    b2c_ap = bass.AP(b2c.tensor, b2c.offset, [[1, 64], [1, 1]])
    P.dma("sp", b2col[0:64, :], b2c_ap, writes=["b2col"])
    P.dma("sp", b2col[64:128, :], b2c_ap, writes=["b2col"])
    P.dma("pool", w2sb[:], dr["w2"], writes=["w2sb"])
    P.dma("pool", pesb[0:64], dr["pe"], writes=["pesb"])
    P.dma("pool", pesb[64:128], dr["pe"], writes=["pesb"])
    for h in range(8):
        P.dma("sp", bT[64 * (h % 2):64 * (h % 2) + 64, h // 2, :], bcast_rows(dr["bs"][h:h + 1, :], 64), writes=["bT"])
    for g in range(2):
        P.dma("pool", KS[g][64 * (1 - g):64 * (1 - g) + 64, :], dr["c_erows"], writes=[("KS", g)])
    for g in range(2):
        P.dma("pool", VCF[:, g, :, 64:126], dr["c_vcf_static"], writes=["VCF"])
    P.op("dve", lambda e: e.memset(Vs[:, :, :, 64:65], 1.0), writes=["Vs"])
    P.op("dve", lambda e: e.memset(Vw[:, :, :, 64:65], 1.0), writes=["Vw"])
    P.op("dve", lambda e: e.memset(hidT[:], 0.0), writes=["hidT"])
    P.op("dve", lambda e: e.memset(kcT[:], 0.0), writes=["kcT"])
    for g in range(2):
        P.op("dve", lambda e, g=g: e.memset(NS[g][:], 0.0), writes=[("NS", g)])
    for kv in range(2):
        P.op("dve", lambda e, kv=kv: e.memset(csrc[kv][:, 0:16], 0.0), writes=[("csrc", kv)])
    P.dma("sp", Fr[4][:, 0:128], dr["c_tril"], writes=[("Fr", 4)])
    for hh in range(2):
        st = Fr[hh][:].rearrange("p (a t) -> p a t", a=4)
        P.dma("sp", st, dr["wst"][:, 4 * hh:4 * hh + 4, :], writes=[("Fr", hh)])
        TT(WsT[:, 4 * hh:4 * hh + 4, :], st, Fr[4][:, 0:128].unsqueeze(1).to_broadcast([128, 4, 128]), ALU.mult,
           [("Fr", hh), ("Fr", 4)], ["WsT"])
    for kv in range(2):
        P.dma("sp", hbias[:, kv, :], bcast_rows(dr["b1"][kv:kv + 1, :], 128), writes=[("hbias0", kv)])

    scr = {}
    seen = set()

    def wload(dst, name, key, sl, res, wt=True):
        resl = res if isinstance(res, list) else [res]
        if name not in scr:
            scr[name] = nc.dram_tensor("scr_" + name, list(shapes[name]), BF16).ap()
        k = (name, key)
        if k not in seen:
            P.dma("pool", dst, sl(dr[name]), writes=resl)
            if wt:
                seen.add(k)
                P.dma("sp", sl(scr[name]), dst, reads=resl, writes=[("scr", k)])
        else:
            P.dma("sp", dst, sl(scr[name]), reads=[("scr", k)], writes=resl)

    def w1view(i):
        return wA[i][:].rearrange("p a b -> p (a b)").rearrange("p (j c) -> p j c", j=4)

    def load_w1(kv, jq):
        i = ring("wA", 6)
        sl = lambda a: a[kv, :, 4 * jq:4 * jq + 4, :]
        first = ("w1", (kv, jq)) not in seen
        v_ = w1view(i)
        if first:
            wload(v_[64:128, :, :], "w1", (kv, jq), sl, ("wA", i), wt=False)
        wload(v_[0:64, :, :], "w1", (kv, jq), sl, ("wA", i))
        if not first:
            wload(v_[64:128, :, :], "w1", (kv, jq), sl, ("wA", i))
        return i

    def setup_tables():
        P.dma("sp", rbx[0:32, :], dr["rbt"], writes=["rbx"])
        P.op("dve", lambda e: e.memset(rbx[32:33, :], 1.0), reads=[], writes=["rbx1"])
        ohs = ohsb[:, :]
        P.dma("sp", ohs, dr["c_oh"], writes=["ohsb"])
        for h in range(8):
            bk = psb[3 + h % 2]
            MM(bk[:, :], rbx[:, h:h + 1].to_broadcast([33, 128]), ohs, True, True, ["rbx", "rbx1", "ohsb"], [("ps", 3 + h % 2)])
            fi = ring("Frs", 3)
            P.op("act", lambda e, bk=bk, fi=fi: e.copy(out=Fr[fi][:], in_=bk[:, :]), [("ps", 3 + h % 2)], [("Fr", fi)])
            P.dma("sp", fsc.ap()[h], Fr[fi][:], reads=[("Fr", fi)], writes=[("fsc", h)])
            yield
        for g in range(2):
            for (tab, off, step, np_) in ((TD, 256, 511, 128), (TS1, 384, 511, 128), (TCn, 353, 496, 16)):
                fi = ring("Frs", 3)
                for hh in range(4):
                    h = 4 * g + hh
                    P.dma("sp", Fr[fi][0:np_, 128 * hh:128 * hh + 128], bass.AP(fsc, h * 65536 + off, [[step, np_], [1, 128]]),
                          reads=[("fsc", h)], writes=[("Fr", fi)])
                src = Fr[fi][0:np_, :].rearrange("p (h q) -> p h q", h=4)
                TS(tab[0][0:np_, g, :, :], src, 8.0, None, ALU.mult, None, [("Fr", fi)], ["tabs"])
                STT(tab[1][0:np_, g, :, :], src, 8.0, tab[0][0:np_, g, :, :], ALU.mult, ALU.subtract, [("Fr", fi), "tabs"], ["tabs"])
                yield


    def setup_hbias():
        for kv in range(2):
            for jq in range(8):
                i = load_w1(kv, jq)
                for jj in range(4):
                    j = 4 * jq + jj
                    MM(psb[6][:, 0:256], pesb[0:64, kv, j:j + 1].to_broadcast([64, 128]), w1view(i)[0:64, jj, :],
                       j == 0, j == 31, [("wA", i), "pesb"], [("ps", 6)])
                yield
            TT(hbias[:, kv, :], psb[6][:, 0:256], hbias[:, kv, :], ALU.add, [("ps", 6), ("hbias0", kv)], [("hbias", kv)])


    NFR, NPT = 5, 6
    WL, WH, WN = 0.6, 2.0, 7.0
    x1scr = nc.dram_tensor("x1scr", [2, 128, 8, TB], F32).ap()
    G_, U_, D_ = 0, 1, 2
    SB_ = (3, 4)
    OC_, OWS_ = 5, 6

    def norm_fm(gname, dst_is_h):
        for k in range(8):
            ACT(actT[:, k, :], xb[:, k, :], AF.Square, [("xb", k)], [("act", k)])
        for k in range(8):
            MM(psb[D_][:, :], onesm[:, :], actT[:, k, :], k == 0, k == 7, [("act", k), "onesm"], [("ps", D_)])
        ACT(rstdb[:], psb[D_][:, :], AF.Sqrt, [("ps", D_)], ["rstdb"], scale=1.0 / D, bias=EPS)
        DV("reciprocal", ["rstdb"], ["rstdb"], out=rstdb[:], in_=rstdb[:])
        yield WN
        for k in range(8):
            if dst_is_h:
                STT(hT[:, k, :], xb[:, k, :], gcols[gname][:, k:k + 1], rstdb[:], ALU.mult, ALU.mult,
                    [("xb", k), "rstdb", gname], [("hT", k)])
            else:
                STT(xb[:, k, :], xb[:, k, :], gcols[gname][:, k:k + 1], rstdb[:], ALU.mult, ALU.mult,
                    [("xb", k), "rstdb", gname], [("xb", k)])
        yield WN

    def load_wA(name, idx):
        i = ring("wA", 6)
        wload(wA[i][:], name, idx, lambda a: a[idx], ("wA", i))
        return i

    def ffn(tag):
        for f in range(NF):
            ig = load_wA("wg" + tag, f)
            iu = load_wA("wu" + tag, f)
            for k in range(8):
                MM(psb[G_][:, :], wA[ig][:, k, :], hT[:, k, :], k == 0, k == 7, [("wA", ig), ("hT", k)], [("ps", G_)])
            yield WL
            for k in range(8):
                MM(psb[U_][:, :], wA[iu][:, k, :], hT[:, k, :], k == 0, k == 7, [("wA", iu), ("hT", k)], [("ps", U_)])
            fi = ring("Fr", NFR)
            ACT(Fr[fi][:], psb[G_][:, :], AF.Tanh, [("ps", G_)], [("Fr", fi)], scale=0.5)
            STT(Fr[fi][:], Fr[fi][:], 1.0, psb[G_][:, :], ALU.add, ALU.mult, [("Fr", fi), ("ps", G_)], [("Fr", fi)])
            STT(actT[:, f, :], Fr[fi][:], 0.5, psb[U_][:, :], ALU.mult, ALU.mult, [("Fr", fi), ("ps", U_)], [("act", f)])
            yield WL
        for d in range(8):
            i = ring("wD", 2)
            wload(wD[i][:], "wd" + tag, d, lambda a, d=d: a[d], ("wD", i))
            po = (D_, G_, U_)[d % 3]
            for f in range(NF):
                MM(psb[po][:, :], wD[i][:, f, :], actT[:, f, :], f == 0, f == NF - 1, [("wD", i), ("act", f)], [("ps", po)])
                if f % 8 == 7:
                    yield WL
            STT(xb[:, d, :], psb[po][:, :], 0.5, xb[:, d, :], ALU.mult, ALU.add, [("ps", po), ("xb", d)], [("xb", d)])
            yield WL

    def rstd_bd(ps_ap, ps_r, n, scale):
        ACT(sqb[:, 0:n], ps_ap, AF.Square, [ps_r], ["sqb"])
        MM(psb[D_][:, 0:n], bdm[:, :], sqb[:, 0:n], True, True, ["sqb", "bdm"], [("ps", D_)])
        fi = ring("Fr", NFR)
        ACT(Fr[fi][:, 0:n], psb[D_][:, 0:n], AF.Sqrt, [("ps", D_)], [("Fr", fi)], scale=scale, bias=EPS)
        DV("reciprocal", [("Fr", fi)], [("Fr", fi)], out=Fr[fi][:, 0:n], in_=Fr[fi][:, 0:n])
        return fi

    class Pipe:
        def __init__(self):
            self.pend_e = None
            self.q = []
            self.n = 0

        def push(self, lhsT, rhs, m, table, reads, pv):
            sp_ = SB_[self.n % 2]
            self.n += 1
            MM(psb[sp_][0:m, :], lhsT, rhs, True, table is None, reads, [("ps", sp_)])
            if table is not None:
                o3 = psb[sp_][0:m, :].rearrange("p (h q) -> p h q", h=4)
                for ti, tb_ in enumerate(table):
                    MM(o3, ident[0:m, 0:m], tb_, False, ti == len(table) - 1, ["ident", "tabs", "TW4"], [("ps", sp_)])
            self._exp()
            self.pend_e = (sp_, m, pv)

        def _exp(self):
            if self.pend_e is not None:
                sp_, m, pv = self.pend_e
                self.pend_e = None
                pi = ring("PT", NPT)
                ACT(PT[pi][0:m, :], psb[sp_][0:m, :], AF.Exp, [("ps", sp_)], [("PT", pi)], scale=0.125)
                self.q.append((pv, pi))

        def pv(self):
            if len(self.q) > 1:
                pv, pi = self.q.pop(0)
                pv(pi)

        def flush(self):
            self._exp()
            while self.q:
                pv, pi = self.q.pop(0)
                pv(pi)

    pipe = Pipe()

    def stage_A(b):
        t0 = b * TB
        tsl = slice(t0, t0 + TB)
        bb = b % 2
        kwsl = slice(t0 % 2048, t0 % 2048 + TB)
        for k in range(8):
            P.dma("sp", xb[:, k, :], dr["xT"][128 * k:128 * k + 128, tsl], writes=[("xb", k)])
        for g in range(2):
            P.dma("pool", VCN[:, bb, g, :, 64:126], dr["c_vcn_static"][:, 4 * b:4 * b + 4, :], writes=[("VCN", bb)])
        yield from norm_fm("ffn1_norm_g", True)
        yield from ffn("1")
        yield from norm_fm("mix_norm_g", True)
        for k in range(8):
            P.dma("sp", x1scr[bb, :, k, :], xb[:, k, :], reads=[("xb", k)], writes=[("x1scr", bb, k)])
        for j in range(12):
            iw = load_wA("wfm", j)
            pb_ = (G_, U_)[j % 2]
            bank = psb[pb_]
            for k in range(8):
                MM(bank[:, :], wA[iw][:, k, :], hT[:, k, :], k == 0, k == 7, [("wA", iw), ("hT", k)], [("ps", pb_)])
            if j < 4:
                ACT(uT[:, j, :], bank[:, :], AF.Gelu_apprx_tanh, [("ps", pb_)], [("uT", j)])
            elif j < 8:
                i = j - 4
                fi = rstd_bd(bank[:, :], ("ps", pb_), TB, 1.0 / 64)
                for g in range(2):
                    rs = slice(64 * g, 64 * g + 64)
                    STT(QS[rs, bb, g, :, i, :], bank[rs, :].rearrange("p (c q) -> p c q", c=4), qgc[rs, 0:1],
                        Fr[fi][rs, :].rearrange("p (c q) -> p c q", c=4), ALU.mult, ALU.mult,
                        [("ps", pb_), ("Fr", fi), "qgc"], [("QS", bb, g)])
            elif j < 10:
                kv = j - 8
                P.op("act", lambda e, kv=kv, bank=bank: e.copy(out=csrc[kv][:, 16:16 + TB], in_=bank[:, :]),
                     [("ps", pb_)], [("csrc", kv)])
            elif j == 10:
                fi = rstd_bd(bank[:, :], ("ps", pb_), TB, 1.0 / 64)
                for g in range(2):
                    rs = slice(64 * g, 64 * g + 64)
                    STT(KS[g][rs, tsl], bank[rs, :], kgc[rs, 1:2], Fr[fi][rs, :], ALU.mult, ALU.mult,
                        [("ps", pb_), ("Fr", fi), "kgc"], [("KS", g)])
            else:
                fi = rstd_bd(bank[:, :], ("ps", pb_), TB, 1.0 / 64)
                STT(KW[:, kwsl], bank[:, :], kgc[:, 2:3], Fr[fi][:, :], ALU.mult, ALU.mult,
                    [("ps", pb_), ("Fr", fi), "kgc"], ["KW"])
            yield
        wtv = actT[:, 8:16, :].rearrange("p a t -> p (a t)").rearrange("p (k n) -> p k n", k=8)
        wtr = actT[:, 16:21, :].rearrange("p a t -> p (a t)")[:, 0:8 * 280].rearrange("p (k n) -> p k n", k=8)
        wtv_res = [("act", k) for k in range(8, 16)]
        wtr_res = [("act", k) for k in range(16, 21)]
        wload(wtv, "wtm_v", 0, lambda a: a, wtv_res)
        wload(wtr, "wtm_r", 0, lambda a: a, wtr_res)
        for ci in range(4):
            c = 4 * b + ci
            qs = slice(128 * ci, 128 * ci + 128)
            for k in range(8):
                MM(psb[G_][:, :], hT[:, k, qs], wtv[:, k, :], k == 0, k == 7, [("hT", k)] + wtv_res, [("ps", G_)])
            yield 1.0
            for k in range(8):
                MM(psb[U_][:, 0:280], hT[:, k, qs], wtr[:, k, :], k == 0, k == 7, [("hT", k)] + wtr_res, [("ps", U_)])
            fi = ring("Fr", NFR)
            ACT(Fr[fi][:], psb[G_][:, :], AF.Gelu_apprx_tanh, [("ps", G_)], [("Fr", fi)])
            f2 = ring("Fr", NFR)
            ACT(Fr[f2][:], Fr[fi][:], AF.Square, [("Fr", fi)], [("Fr", f2), "ssv"], accum_out=small[:, 0:1])
            ACT(small[:, 1:2], small[:, 0:1], AF.Sqrt, ["ssv"], ["rsv"], scale=1.0 / 512, bias=EPS)
            DV("reciprocal", ["rsv"], ["rsv2"], out=small[:, 2:3], in_=small[:, 1:2])
            STT(vnb[:], Fr[fi][:], small[:, 2:3], gvrow[:], ALU.mult, ALU.mult, [("Fr", fi), "rsv2", "gvrow"], ["vnb"])
            P.op("act", lambda e, c=c: e.copy(out=Vs[:, :, c, 0:64], in_=psb[U_][:, 0:128].rearrange("p (g d) -> p g d", g=2)),
                 [("ps", U_)], ["Vs"])
            P.op("act", lambda e, c=c: e.copy(out=Vw[:, :, c % 16, 0:64], in_=psb[U_][:, 128:256].rearrange("p (g d) -> p g d", g=2)),
                 [("ps", U_)], ["Vw"])
            ACT(gates[:, bb, ci, :], psb[U_][:, 256:280], AF.Tanh, [("ps", U_)], [("gates", bb)], scale=0.5)
            TS(gates[:, bb, ci, :], gates[:, bb, ci, :], 0.5, 0.5, ALU.mult, ALU.add, [("gates", bb)], [("gates", bb)])
            yield WH
            yield WH
            for jj in range(4):
                for hh in range(2):
                    h = 2 * jj + hh
                    MM(psb[D_][64 * hh:64 * hh + 64, 128 * jj:128 * jj + 128], vnb[:, 64 * h:64 * h + 64], WsT[:, h, :],
                       True, True, ["vnb", "WsT"], [("ps", D_)], skip_group_check=True)
            f3 = ring("Fr", NFR)
            TT(Fr[f3][:], psb[D_][:, :], bT[:].rearrange("p a t -> p (a t)"), ALU.add, [("ps", D_), "bT"], [("Fr", f3)])
            TT(yT[:, bb, 0:4, qs], Fr[f3][:].rearrange("p (a t) -> p a t", a=4), uT[:, :, qs], ALU.mult,
               [("Fr", f3)] + [("uT", j) for j in range(4)], [("yT", bb, ci)])
            yield WH
        n0 = 32 * b - 1 if b > 0 else 0
        n1 = 32 * b + 30
        N = n1 - n0 + 1
        base_col = 16 * n0 - (t0 - 16)
        for kv in range(2):
            for jq in range(8):
                i = load_w1(kv, jq)
                for jj in range(4):
                    j = 4 * jq + jj
                    for g in range(2):
                        rs = slice(64 * g, 64 * g + 64)
                        src = csrc[kv][rs, base_col + j: base_col + j + 16 * (N - 1) + 1: 16]
                        MM(psb[(G_, U_)[g]][0:N, 0:256], src, w1view(i)[rs, jj, :], j == 0, j == 31,
                           [("csrc", kv), ("wA", i)], [("ps", (G_, U_)[g])])
                yield 1.5
            for g in range(2):
                pg_ = (G_, U_)[g]
                TT(hid32[0:N, :], psb[pg_][0:N, 0:256], hbias[0:N, kv, :], ALU.add, [("ps", pg_), ("hbias", kv)], ["hid32"])
                ACT(hidb[0:N, :], hid32[0:N, :], AF.Gelu_apprx_tanh, ["hid32"], ["hidb"])
                for ch in range(2):
                    TR(ps7[:, 32 * ch:32 * ch + N], hidb[0:N, 128 * ch:128 * ch + 128], ["hidb"], [("ps", 7)])
                P.op("act", lambda e, kv=kv, g=g, N=N, n0=n0: e.copy(
                    out=hidT[:, kv, g, :, 8 + n0:8 + n0 + N], in_=ps7[:, 0:64].rearrange("p (c n) -> p c n", c=2)[:, :, 0:N]),
                    [("ps", 7)], ["hidT"])
                yield WH
        for kv in range(2):
            P.op("dve", lambda e, kv=kv: e.tensor_copy(out=csrc[kv][:, 0:16], in_=csrc[kv][:, TB:TB + 16]),
                 [("csrc", kv)], [("csrc", kv)])
        for g in range(2):
            for ch in range(2):
                MM(psb[D_][64 * g:64 * g + 64, 0:N], w2sb[:, 0, ch, :], hidT[:, 0, g, ch, 8 + n0:8 + n0 + N],
                   ch == 0, ch == 1, ["w2sb", "hidT"], [("ps", D_)], skip_group_check=True)
        TS(kc32[:, 0:N], psb[D_][:, 0:N], b2col[:, 0:1], None, ALU.add, None, [("ps", D_), "b2col"], ["kc32"])
        ACT(kcsq[:, 0:N], kc32[:, 0:N], AF.Square, ["kc32"], ["kcsq"])
        MM(psb[D_][:, 0:N], bdm[:, :], kcsq[:, 0:N], True, True, ["kcsq", "bdm"], [("ps", D_)])
        ACT(kcr[:, 0:N], psb[D_][:, 0:N], AF.Sqrt, [("ps", D_)], ["kcr"], scale=1.0 / 64, bias=EPS)
        DV("reciprocal", ["kcr"], ["kcr"], out=kcr[:, 0:N], in_=kcr[:, 0:N])
        STT(kcT[:, 8 + n0:8 + n0 + N], kc32[:, 0:N], kgc[:, 0:1], kcr[:, 0:N], ALU.mult, ALU.mult, ["kc32", "kcr", "kgc"], ["kcT"])
        yield WH
        nts = sorted({n0 // 128, n1 // 128})
        for g in range(2):
            for nt in nts:
                for ch in range(2):
                    MM(psb[D_][:, 0:64], hidT[:, 1, g, ch, 8 + 128 * nt:8 + 128 * nt + 128], w2sb[:, 1, ch, :],
                       ch == 0, ch == 1, ["hidT", "w2sb"], [("ps", D_)])
                TT(VCF[:, g, nt, 0:64], psb[D_][:, 0:64], b2row[:, :], ALU.add, [("ps", D_), "b2row"], ["VCF"])
            for ci in range(4):
                c = 4 * b + ci
                for ch in range(2):
                    MM(psb[D_][0:16, 0:64], hidT[:, 1, g, ch, 8 * c:8 * c + 16], w2sb[:, 1, ch, :],
                       ch == 0, ch == 1, ["hidT", "w2sb"], [("ps", D_)])
                TT(VCN[:, bb, g, ci, 0:64], psb[D_][0:16, 0:64], b2row[0:16, :], ALU.add, [("ps", D_), "b2row"], [("VCN", bb)])
                if c == 0:
                    P.op("dve", lambda e, g=g: e.memset(VCN[0:8, 0, g, 0, 0:64], 0.0), [("VCN", 0)], [("VCN", 0)])
            yield WH

    def stage_B(b):
        bb = b % 2
        P.dma("sp", selm[:], dr["c_selmask"][b], writes=["selm"])
        for ci in range(4):
            c = 4 * b + ci
            for g in range(2):
                qr = slice(64 * g, 64 * g + 64)
                sr = slice(64 * (1 - g), 64 * (1 - g) + 64)
                rhs_q = QS[qr, bb, g, ci, :, :].rearrange("p h q -> p (h q)")
                rhs_full = QS[:, bb, g, ci, :, :].rearrange("p h q -> p (h q)")
                gsl = lambda br: gates[:, bb, ci, br * 8 + g * 4: br * 8 + g * 4 + 4]
                QR = ("QS", bb, g)
                oc3 = psb[OC_][:, 0:504].rearrange("p (h x) -> p h x", h=4)

                stC = [True]

                def pv_c(pi, m, nt):
                    for h in range(4):
                        vrhs = VCN[0:16, bb, g, ci, :] if nt is None else VCF[0:m, g, nt, :]
                        MM(psb[OC_][:, 126 * h:126 * h + 126], PT[pi][0:m, 128 * h:128 * h + 128], vrhs,
                           stC[0], False, [("PT", pi), ("VCN", bb), "VCF"], [("ps", OC_)], skip_group_check=True)
                        stC[0] = False

                tiles = [(None, 16, 8 * c, [TCn[0][:, g, :, :], TCn[1][:, g, :, :]])]
                nf = 8 * c - 8
                if nf > 0:
                    tiles.append((0, min(nf, 128), 8, None))
                if nf > 128:
                    tiles.append((1, nf - 128, 8 + 128, None))
                for (nt, m, col, table) in tiles:
                    pipe.push(kcT[qr, col:col + m], rhs_q, m, table, ["kcT", QR],
                              lambda pi, m=m, nt=nt: pv_c(pi, m, nt))
                    yield
                    pipe.pv()

                def make_pv(Vt, kt, ringw, state):
                    def pv(pi):
                        for h in range(4):
                            vrhs = Vt[:, g, kt % 16, :] if ringw else Vt[:, g, kt, :]
                            MM(psb[OWS_][:, 65 * h:65 * h + 65], PT[pi][:, 128 * h:128 * h + 128], vrhs,
                               state[0], False, [("PT", pi), "Vs", "Vw"], [("ps", OWS_)], skip_group_check=True)
                            state[0] = False
                    return pv

                def table_for(kt, br):
                    if kt == c:
                        return [TD[0][:, g, :, :], TD[1][:, g, :, :]]
                    if kt == c - 1:
                        return [TS1[0][:, g, :, :], TS1[1][:, g, :, :]]
                    if br == 2 and kt == c - 4:
                        return [TW4[:, :].unsqueeze(1).to_broadcast([128, 4, 128])]
                    return None

                wk = [kt for kt in range(c - 4, c + 1) if kt >= 0]
                stW = [True]
                chain_it = None
                def chain_gen():
                    TS(small[:, 8:12], oc3[:, :, 64], 1e-30, None, ALU.max, None, [("ps", OC_)], ["rlc"])
                    DV("reciprocal", ["rlc"], ["rlc2"], out=small[:, 12:16], in_=small[:, 8:12])
                    for h in range(4):
                        src = psb[OC_][:, 126 * h + 65:126 * h + 126]
                        if h == 0:
                            STT(impb[:], src, small[:, 12:13], selm[:, ci, :], ALU.mult, ALU.add,
                                [("ps", OC_), "rlc2", "selm"], ["impb"])
                        else:
                            STT(impb[:], src, small[:, 12 + h:13 + h], impb[:], ALU.mult, ALU.add,
                                [("ps", OC_), "rlc2", "impb"], ["impb"])
                        if h == 1:
                            yield
                    DV("max", ["impb"], ["m8a"], out=m8a[:], in_=impb[:])
                    DV("match_replace", ["m8a", "impb"], ["impw"], out=impw[:], in_to_replace=m8a[:], in_values=impb[:], imm_value=-3e9)
                    yield
                    DV("max", ["impw"], ["m8b"], out=m8b[:], in_=impw[:])
                    if c < 31:
                        TS(NS[g][:, sr.start + 1:sr.start + 62], impb[:], m8b[:, 6:7], -BIG, ALU.is_lt, ALU.mult,
                           ["impb", "m8b"], [("NS", g)])
                    else:
                        TS(NS[g][0:64, sr.start + 1:sr.start + 62], impb[0:64, :], m8b[0:64, 5:6], -BIG, ALU.is_lt, ALU.mult,
                           ["impb", "m8b"], [("NS", g)])
                        TS(NS[g][64:128, sr.start + 1:sr.start + 62], impb[64:128, :], m8b[64:128, 4:5], -BIG, ALU.is_lt, ALU.mult,
                           ["impb", "m8b"], [("NS", g)])
                    yield
                    TT(small[:, 16:20], gsl(0), small[:, 12:16], ALU.mult, [("gates", bb), "rlc2"], ["coef"])
                    TT(yb[:, 256 * g:256 * g + 256].rearrange("p (h d) -> p h d", h=4), oc3[:, :, 0:64],
                       small[:, 16:20].unsqueeze(2).to_broadcast([128, 4, 64]), ALU.mult,
                       [("ps", OC_), "coef"], [("yb", g)])

                for wi, kt in enumerate(wk):
                    kc0 = (128 * kt) % 2048
                    pipe.push(KW[qr, kc0:kc0 + 128], rhs_q, 128, table_for(kt, 2), ["KW", QR], make_pv(Vw, kt, True, stW))
                    yield
                    pipe.pv()
                    if wi == 1:
                        chain_it = chain_gen()
                        next(chain_it, None)
                    if chain_it is not None:
                        next(chain_it, None)
                    yield
                if chain_it is None:
                    pipe.flush()
                    chain_it = chain_gen()
                for _ in chain_it:
                    pass
                pipe.flush()
                TR(ps7[:, 256:384], NS[g][:, :], [("NS", g)], [("ps", 7)])
                P.op("dve", lambda e, g=g, ci=ci, sr=sr: e.tensor_copy(
                    out=QS[sr, bb, g, ci, :, :], in_=ps7[sr, 256:384].unsqueeze(1).to_broadcast([64, 4, 128])),
                    [("ps", 7)], [QR])

                def combine(br):
                    o3 = psb[OWS_][:, 0:260].rearrange("p (h x) -> p h x", h=4)
                    DV("reciprocal", [("ps", OWS_)], ["rl"], out=small[:, 20:24], in_=o3[:, :, 64])
                    TT(small[:, 24:28], gsl(br), small[:, 20:24], ALU.mult, [("gates", bb), "rl"], ["coef2"])
                    TT(vtmp[:], o3[:, :, 0:64], small[:, 24:28].unsqueeze(2).to_broadcast([128, 4, 64]), ALU.mult,
                       [("ps", OWS_), "coef2"], ["vtmp"])
                    ysl = yb[:, 256 * g:256 * g + 256]
                    if br == 2:
                        TT(ysl, ysl, vtmp[:].rearrange("p h d -> p (h d)"), ALU.add, ["vtmp", ("yb", g)], [("yb", g)])
                    else:
                        TT(ybb[:, 256 * g:256 * g + 256], ysl, vtmp[:].rearrange("p h d -> p (h d)"), ALU.add,
                           ["vtmp", ("yb", g)], [("ybb", g)])

                combine(2)
                yield
                sk = [kt for kt in (c - 1, c) if kt >= 0] + list(range(0, c - 1))
                stS = [True]
                for kt in sk:
                    ksl = slice(128 * kt, 128 * kt + 128)
                    pipe.push(KS[g][:, ksl], rhs_full, 128, table_for(kt, 1), [("KS", g), QR], make_pv(Vs, kt, False, stS))
                    yield
                    pipe.pv()
                pipe.flush()
                combine(1)
                yield
            for jj in range(4):
                TR(ps7[:, 512 + 128 * jj:512 + 128 * jj + 128], ybb[:, 128 * jj:128 * jj + 128],
                   [("ybb", 0), ("ybb", 1)], [("ps", 7)])
            P.op("dve", lambda e, ci=ci: e.tensor_copy(out=yT[:, bb, 4:8, 128 * ci:128 * ci + 128],
                                                          in_=ps7[:, 512:1024].rearrange("p (a q) -> p a q", a=4)),
                 [("ps", 7)], [("yT", bb, ci)])
            yield

    def stage_C(b):
        t0 = b * TB
        tsl = slice(t0, t0 + TB)
        bb = b % 2
        for k in range(8):
            P.dma("sp", xb[:, k, :], x1scr[bb, :, k, :], reads=[("x1scr", bb, k)], writes=[("xb", k)])
        for d in range(8):
            iw = load_wA("wo", d)
            pb_ = (G_, U_, D_)[d % 3]
            for k in range(8):
                MM(psb[pb_][:, :], wA[iw][:, k, :], yT[:, bb, k, :], k == 0, k == 7,
                   [("wA", iw)] + [("yT", bb, ci) for ci in range(4)], [("ps", pb_)])
            TT(xb[:, d, :], psb[pb_][:, :], xb[:, d, :], ALU.add, [("ps", pb_), ("xb", d)], [("xb", d)])
            yield
        yield from norm_fm("ffn2_norm_g", True)
        yield from ffn("2")
        yield from norm_fm("final_norm_g", False)
        for k in range(8):
            P.dma("sp", outT[128 * k:128 * k + 128, tsl], xb[:, k, :], reads=[("xb", k)], writes=[("outT", k)])
        yield

    def chain(*gens):
        for g_ in gens:
            yield from g_

    def n_att_units(b):
        n = 0
        for c in range(4 * b, 4 * b + 4):
            nC = 1 + (1 if 8 * c - 8 > 0 else 0) + (1 if 8 * c - 8 > 128 else 0)
            nW = min(5, c + 1)
            n += 2 * (nC + nW + (c + 1) + 2) + 1
        return n

    def interleave(dense, att, n_d, n_a):
        acc = 0.0
        ratio = n_a / max(n_d, 1)
        d_alive = dense is not None
        a_alive = att is not None
        while d_alive or a_alive:
            w = 1.0
            if d_alive:
                try:
                    w = next(dense)
                    w = 1.0 if w is None else w
                except StopIteration:
                    d_alive = False
            if a_alive:
                acc += ratio * w if d_alive else 1.0
                while acc >= 1.0 and a_alive:
                    acc -= 1.0
                    try:
                        next(att)
                    except StopIteration:
                        a_alive = False

    def precast_ffn2():
        def one(name, key, sl, npc=8):
            if name not in scr:
                scr[name] = nc.dram_tensor("scr_" + name, list(shapes[name]), BF16).ap()
            i = ring("pcs", 2)
            P.dma("pool", pcs[i][:, 0:npc, :], sl(dr[name]), writes=[("pcs", i)])
            P.dma("pool", sl(scr[name]), pcs[i][:, 0:npc, :], reads=[("pcs", i)], writes=[("scr", (name, key))])
        for d in range(8):
            one("wo", d, lambda a, d=d: a[d])
            seen.add(("wo", d))
            yield
        for f in range(NF):
            for nm in ("wg2", "wu2"):
                one(nm, f, lambda a, f=f: a[f])
                seen.add((nm, f))
                yield
        for d in range(8):
            for (f0, f1) in ((0, 8), (8, 16), (16, NF)):
                one("wd2", d, lambda a, d=d, f0=f0, f1=f1: a[d][:, f0:f1, :], npc=f1 - f0)
                yield
            seen.add(("wd2", d))

    def zip_gens(a, b_):
        alive = [a, b_]
        while alive:
            for g_ in list(alive):
                try:
                    next(g_)
                    yield
                except StopIteration:
                    alive.remove(g_)

    interleave(stage_A(0), chain(setup_hbias(), setup_tables()), 40, 80)
    for b in range(NBLK):
        parts = []
        n_d = 0
        if b >= 1:
            parts.append(stage_C(b - 1))
            n_d += 8 + 4 * WN + 68 * WL + 1
        if b + 1 < NBLK:
            parts.append(stage_A(b + 1))
            n_d += 4 * WN + 68 * WL + 12 + 4 * (1 + 3 * WH) + 24 + 4 * WH + WH + 2 * WH
        if b == 0:
            interleave(chain(*parts), zip_gens(stage_B(0), precast_ffn2()), n_d, 2 * 76)
        else:
            interleave(chain(*parts) if parts else None, stage_B(b), n_d, n_att_units(b))
    for _ in stage_C(NBLK - 1):
        pass

    print('SBUF bytes remaining/partition:', nc.sbuf_bytes_remaining)
    P.emit()
    es.close()
    return nc, list(dbg.keys())


_CACHE = {}


def kernel(**inputs):
    inp = {k: np.asarray(v) for k, v in inputs.items()}
    x = inp["x"]
    w = _prep_weights(inp)
    w.update(_static_consts())
    shapes = {k: v.shape for k, v in w.items()}
    shapes["xT"] = (D, T)
    key = "nc"
    if key not in _CACHE:
        _CACHE[key] = build(shapes)
    nc, dbgk = _CACHE[key]
    in_maps = []
    for bi in range(8):
        m = dict(w)
        m["xT"] = np.ascontiguousarray(x[bi].T.astype(np.float32))
        in_maps.append(m)
    res = run_bass_kernel_spmd(nc, in_maps, core_ids=list(range(8)))
    out = np.stack([np.ascontiguousarray(res.results[bi]["outT"].T) for bi in range(8)], 0).astype(np.float32)
    if DEBUG:
        kernel.dbg = {k: res.results[0]["dbg_" + k] for k in dbgk}
    return out
```

```python
import math
from contextlib import ExitStack
import numpy as np
import concourse.bass as bass
import concourse.mybir as mybir
from concourse.bass_utils import run_bass_kernel_spmd

F32 = mybir.dt.float32
BF16 = mybir.dt.bfloat16
AF = mybir.ActivationFunctionType
ALU = mybir.AluOpType

ENGS = ("pe", "act", "dve", "pool", "sp")
T = 4096
D = 1024
DFF = 2816
NF = 22
TB = 512
NBLK = T // TB
EPS = 1e-6
BIG = 30000.0
DEBUG = False
SCHED_DEBUG = False


class Prog:
    N_DMA_SEMS = 12

    def __init__(self, nc):
        self.nc = nc
        self.ops = []
        self.last_write = {}
        self.readers = {}

    def op(self, eng, fn, reads=(), writes=(), dma=False):
        i = len(self.ops)
        deps = set()
        for r in reads:
            w = self.last_write.get(r)
            if w is not None:
                deps.add(w)
        for w_ in writes:
            w = self.last_write.get(w_)
            if w is not None:
                deps.add(w)
            for rd in self.readers.get(w_, ()):
                deps.add(rd)
        deps.discard(i)
        self.ops.append(dict(eng=eng, fn=fn, deps=deps, dma=dma, marked=dma))
        for r in reads:
            self.readers.setdefault(r, []).append(i)
        for w_ in writes:
            self.last_write[w_] = i
            self.readers[w_] = []
        return i

    def dma(self, eng, out, in_, reads=(), writes=(), **kw):
        def fn(e, out=out, in_=in_, kw=kw):
            return e.dma_start(out=out, in_=in_, **kw)
        return self.op(eng, fn, reads, writes, dma=True)

    def emit(self, final_wait_eng="sp"):
        nc = self.nc
        ops = self.ops
        for i, o in enumerate(ops):
            latest = {}
            dd = set()
            for d in o["deps"]:
                od = ops[d]
                if od["dma"]:
                    dd.add(d)
                    continue
                if o["eng"] == "pe" and od["eng"] == "pe":
                    continue
                if latest.get(od["eng"], -1) < d:
                    latest[od["eng"]] = d
            o["deps"] = dd | set(latest.values())
        for o in ops:
            for d in o["deps"]:
                ops[d]["marked"] = True
        with ExitStack() as es:
            esem = {e: es.enter_context(nc.semaphore("s_" + e)) for e in ENGS}
            dsems = {e: [es.enter_context(nc.semaphore("d_%s%d" % (e, k))) for k in range(self.N_DMA_SEMS)]
                     for e in ("sp", "pool")}
            cnt = {e: 0 for e in ENGS}
            dcnt = {e: [0] * self.N_DMA_SEMS for e in dsems}
            drr = {e: 0 for e in dsems}
            for o in ops:
                if o["dma"]:
                    e = o["eng"]
                    k = drr[e]
                    drr[e] = (k + 1) % self.N_DMA_SEMS
                    o["prev"] = dcnt[e][k]
                    dcnt[e][k] += 16
                    o["sem"] = dsems[e][k]
                    o["val"] = dcnt[e][k]
                elif o["marked"]:
                    cnt[o["eng"]] += 1
                    o["sem"] = esem[o["eng"]]
                    o["val"] = cnt[o["eng"]]
            by_eng = {e: [o for o in ops if o["eng"] == e] for e in ENGS}
            all_dma = [o for o in ops if o["dma"]]
            self.stats = {e: len(by_eng[e]) for e in ENGS}
            with nc.Block() as block:
                def run(e_name, eng):
                    known = {}

                    def wait_all(pairs):
                        best = {}
                        for sem, val in pairs:
                            if best.get(sem.num, (None, 0))[1] < val:
                                best[sem.num] = (sem, val)
                        for key, (sem, val) in best.items():
                            if known.get(key, 0) >= val:
                                continue
                            known[key] = val
                            eng.wait_ge(sem, val)
                    for o in by_eng[e_name]:
                        pairs = [(ops[d]["sem"], ops[d]["val"]) for d in o["deps"]]
                        if o["dma"] and o["prev"] > 0:
                            pairs.append((o["sem"], o["prev"]))
                        wait_all(pairs)
                        ins = o["fn"](eng)
                        if o["marked"]:
                            ins.then_inc(o["sem"], 16 if o["dma"] else 1)
                    if e_name == final_wait_eng:
                        wait_all([(o["sem"], o["val"]) for o in all_dma])

                @block.tensor
                def _(eng):
                    run("pe", eng)

                @block.scalar
                def _(eng):
                    run("act", eng)

                @block.vector
                def _(eng):
                    run("dve", eng)

                @block.gpsimd
                def _(eng):
                    run("pool", eng)

                @block.sync
                def _(eng):
                    run("sp", eng)


def _t5_bucket_np(d):
    n = np.maximum(d, 0)
    nf = np.maximum(n, 1).astype(np.float32)
    large = 16 + (np.log(nf / np.float32(16.0)) / np.float32(math.log(8.0)) * np.float32(16.0)).astype(np.int32)
    large = np.minimum(large, 31)
    return np.where(n < 16, n, large)


def _static_consts():
    c = {}
    i = np.arange(512)
    d = i - 256
    valid = d >= 0
    bk = _t5_bucket_np(d)
    oh = np.zeros((33, 512), np.float32)
    for b in range(32):
        oh[b] = ((bk == b) & valid).astype(np.float32)
    oh[31] -= valid.astype(np.float32)
    oh[32] = -BIG * (~valid).astype(np.float32)
    c["c_oh"] = oh
    m = np.zeros((32, 128, 2, 64), np.float32)
    for cc in range(32):
        for p in range(128):
            cur = (128 * cc + p) // 64
            j = np.arange(64)
            elig = j <= cur
            forced = (j == 0) | (j == cur) | (j == cur - 1)
            m[cc, p, 0] = np.where(elig, np.where(forced, 1e6, 0.0), -1e9)
    c["c_selmask"] = m[:, :, 0, 1:62].reshape(8, 4, 128, 61).transpose(0, 2, 1, 3).copy()
    p = np.arange(128)[:, None]
    j = np.arange(128)[None, :]
    c["c_tw4"] = np.where(p > j, 0.0, -8.0 * BIG).astype(np.float32)
    s = np.arange(T)
    c["c_erows"] = (s[None, :] // 64 == np.arange(64)[:, None]).astype(np.float32)
    n = np.arange(256)[:, None]
    jb = np.arange(64)[None, :]
    ov = ((16 * n <= 64 * jb + 63) & (16 * n + 31 >= 64 * jb) & (n < 255)).astype(np.float32)
    one = (np.arange(256) < 255).astype(np.float32)[:, None]
    st = np.concatenate([one, ov[:, 1:62]], axis=1)
    c["c_vcf_static"] = st.reshape(2, 128, 62).transpose(1, 0, 2).copy()
    vn = np.zeros((16, 32, 62), np.float32)
    for cc in range(32):
        for pp in range(16):
            nn = 8 * cc - 8 + pp
            if 0 <= nn <= 254:
                vn[pp, cc] = st[nn]
    c["c_vcn_static"] = vn
    c["c_ident"] = np.eye(128, dtype=np.float32)
    bd = np.zeros((128, 128), np.float32)
    bd[:64, :64] = 1.0
    bd[64:, 64:] = 1.0
    c["c_bd"] = bd
    c["c_ones"] = np.ones((128, 128), np.float32)
    c["c_tril"] = (np.arange(128)[:, None] <= np.arange(128)[None, :]).astype(np.float32)
    return c


def _prep_weights(inp):
    f = lambda a: np.ascontiguousarray(a, dtype=np.float32)
    w = {}
    for tag, pre in (("1", "ffn1"), ("2", "ffn2")):
        wg = inp[pre + "_w_gate"][0]
        wu = inp[pre + "_w_up"][0]
        wd = inp[pre + "_w_down"][0]
        w["wg" + tag] = f(wg.reshape(8, 128, NF, 128).transpose(2, 1, 0, 3))
        w["wu" + tag] = f(wu.reshape(8, 128, NF, 128).transpose(2, 1, 0, 3))
        w["wd" + tag] = f(wd.reshape(NF, 128, 8, 128).transpose(2, 1, 0, 3))
    win = inp["w_in"][0]
    cols = []
    cols += [np.arange(j * 128, (j + 1) * 128) for j in range(4)]
    for i in range(4):
        cols.append(np.concatenate([1024 + (0 * 4 + i) * 64 + np.arange(64), 1024 + (1 * 4 + i) * 64 + np.arange(64)]))
    cols.append(1536 + np.arange(128))
    cols.append(1664 + np.arange(128))
    cols.append(1792 + np.arange(128))
    cols.append(2048 + np.arange(128))
    wfm = np.stack([win[:, cidx] for cidx in cols], 0)
    w["wfm"] = f(wfm.reshape(12, 8, 128, 128).transpose(0, 2, 1, 3))
    w["wtm_v"] = f(win[:, 512:1024].reshape(8, 128, 512).transpose(1, 0, 2))
    rcols = np.concatenate([1920 + np.arange(128), 2176 + np.arange(128), 2304 + np.arange(24)])
    w["wtm_r"] = f(win[:, rcols].reshape(8, 128, 280).transpose(1, 0, 2))
    wo = inp["w_out"][0]
    w["wo"] = f(wo.reshape(8, 128, 8, 128).transpose(2, 1, 0, 3))
    w["w1"] = f(inp["cmp_w1"][0].reshape(2, 32, 64, 256).transpose(0, 2, 1, 3))
    w["w2"] = f(inp["cmp_w2"][0].reshape(2, 2, 128, 64).transpose(2, 0, 1, 3))
    w["pe"] = f(inp["cmp_pe"][0].transpose(2, 0, 1))
    w["b1"] = f(inp["cmp_b1"][0])
    w["b2"] = f(inp["cmp_b2"][0])
    for nm in ("ffn1_norm_g", "mix_norm_g", "ffn2_norm_g", "final_norm_g"):
        w[nm] = f(inp[nm][0].reshape(8, 128).T)
    w["gv"] = f(inp["gmlp_v_norm_g"])
    w["wst"] = f(inp["gmlp_w_s"][0].transpose(2, 0, 1))
    w["bs"] = f(inp["gmlp_b_s"][0])
    w["qg"] = f(inp["q_norm_g"][0].reshape(64, 1))
    w["kg"] = f(inp["k_norm_g"][0].T)
    w["rbt"] = f(inp["rel_bias"].T)
    return w


def build(shapes):
    nc = bass.Bass("TRN2", target_bir_lowering=False)
    P = Prog(nc)
    es = ExitStack()
    dr = {}
    for nm, shp in shapes.items():
        dr[nm] = nc.dram_tensor(nm, list(shp), F32, kind="ExternalInput").ap()
    outT = nc.dram_tensor("outT", [D, T], F32, kind="ExternalOutput").ap()
    fsc = nc.dram_tensor("fsc", [8, 128, 512], F32)
    dbg = {}

    def sb(name, shape, dt):
        return es.enter_context(nc.sbuf_tensor(name, shape, dt))

    xb = sb("xb", [128, 8, TB], F32)
    hT = sb("hT", [128, 8, TB], BF16)
    actT = sb("actT", [128, NF, TB], BF16)
    wA = [sb("wA%d" % i, [128, 8, 128], BF16) for i in range(6)]
    wD = [sb("wD%d" % i, [128, NF, 128], BF16) for i in range(2)]
    Fr = [sb("Fr%d" % i, [128, 512], F32) for i in range(5)]
    PT = [sb("PT%d" % i, [128, 512], BF16) for i in range(6)]
    uT = sb("uT", [128, 4, TB], BF16)
    yT = sb("yT", [128, 2, 8, TB], BF16)
    QS = sb("QS", [128, 2, 2, 4, 4, 128], BF16)
    yb = sb("yb", [128, 512], F32)
    ybb = sb("ybb", [128, 512], BF16)
    NS = [sb("NS%d" % g, [128, 128], BF16) for g in range(2)]
    KS = [sb("KS%d" % g, [128, T], BF16) for g in range(2)]
    KW = sb("KW", [128, 2048], BF16)
    Vs = sb("Vs", [128, 2, 32, 65], BF16)
    Vw = sb("Vw", [128, 2, 16, 65], BF16)
    csrc = [sb("csrc%d" % kv, [128, 16 + TB], BF16) for kv in range(2)]
    kcT = sb("kcT", [128, 264], BF16)
    hidT = sb("hidT", [128, 2, 2, 2, 272], BF16)
    VCF = sb("VCF", [128, 2, 2, 126], BF16)
    VCN = sb("VCN", [16, 2, 2, 4, 126], BF16)
    TD = [sb("TD%d" % i, [128, 2, 4, 128], BF16) for i in range(2)]
    TS1 = [sb("TS1%d" % i, [128, 2, 4, 128], BF16) for i in range(2)]
    TCn = [sb("TCn%d" % i, [16, 2, 4, 128], BF16) for i in range(2)]
    TW4 = sb("TW4", [128, 128], BF16)
    gates = sb("gates", [128, 2, 4, 24], F32)
    selm = sb("selm", [128, 4, 61], F32)
    ident = sb("ident", [128, 128], BF16)
    bdm = sb("bdm", [128, 128], BF16)
    onesm = sb("onesm", [128, 128], BF16)
    WsT = sb("WsT", [128, 8, 128], BF16)
    bT = sb("bT", [128, 4, 128], F32)
    gvrow = sb("gvrow", [128, 512], F32)
    hbias = sb("hbias", [128, 2, 256], F32)
    b2row = sb("b2row", [128, 64], F32)
    b2col = sb("b2col", [128, 1], F32)
    w2sb = sb("w2sb", [128, 2, 2, 64], BF16)
    pesb = sb("pesb", [128, 2, 32], BF16)
    gcols = {nm: sb("g_" + nm, [128, 8], F32) for nm in ("ffn1_norm_g", "mix_norm_g", "ffn2_norm_g", "final_norm_g")}
    qgc = sb("qgc", [128, 1], F32)
    kgc = sb("kgc", [128, 3], F32)
    rbx = sb("rbx", [33, 8], F32)
    small = sb("small", [128, 64], F32)
    impb = sb("impb", [128, 61], F32)
    impw = sb("impw", [128, 61], F32)
    m8a = sb("m8a", [128, 8], F32)
    m8b = sb("m8b", [128, 8], F32)
    hid32 = sb("hid32", [32, 256], F32)
    hidb = sb("hidb", [32, 256], BF16)
    kc32 = sb("kc32", [128, 32], F32)
    kcsq = sb("kcsq", [128, 32], BF16)
    kcr = sb("kcr", [128, 32], F32)
    vtmp = sb("vtmp", [128, 4, 64], F32)
    pcs = [sb("pcs%d" % i, [128, 8, 128], BF16) for i in range(2)]
    rstdb = sb("rstdb", [128, 512], F32)
    sqb = sb("sqb", [128, 512], BF16)
    ohsb = sb("ohsb", [33, 512], F32)
    vnb = sb("vnb", [128, 512], BF16)

    psb = [es.enter_context(nc.psum_tensor("psb%d" % i, [128, 512], F32)) for i in range(7)]
    ps7 = es.enter_context(nc.psum_tensor("ps7", [128, 1024], BF16))

    rr = {}

    def ring(name, n):
        i = rr.get(name, 0)
        rr[name] = (i + 1) % n
        return i

    def MM(out, lhsT, rhs, start, stop, reads, writes, **kw):
        return P.op("pe", lambda e: e.matmul(out, lhsT=lhsT, rhs=rhs, start=start, stop=stop, **kw), reads, writes)

    def TR(out, in_, reads, writes):
        return P.op("pe", lambda e: e.transpose(out, in_, ident[0:in_.shape[0], 0:in_.shape[0]]), list(reads) + ["ident"], writes)

    def ACT(out, in_, func, reads, writes, **kw):
        return P.op("act", lambda e: e.activation(out=out, in_=in_, func=func, **kw), reads, writes)

    def DV(name, reads, writes, *a, **kw):
        return P.op("dve", lambda e: getattr(e, name)(*a, **kw), reads, writes)

    def STT(out, in0, scalar, in1, op0, op1, reads, writes):
        return P.op("dve", lambda e: e.scalar_tensor_tensor(out=out, in0=in0, scalar=scalar, in1=in1, op0=op0, op1=op1), reads, writes)

    def TS(out, in0, s1, s2, op0, op1, reads, writes):
        if op1 is None:
            return P.op("dve", lambda e: e.tensor_scalar(out=out, in0=in0, scalar1=s1, scalar2=None, op0=op0), reads, writes)
        return P.op("dve", lambda e: e.tensor_scalar(out=out, in0=in0, scalar1=s1, scalar2=s2, op0=op0, op1=op1), reads, writes)

    def TT(out, in0, in1, op, reads, writes):
        return P.op("dve", lambda e: e.tensor_tensor(out=out, in0=in0, in1=in1, op=op), reads, writes)

    def bcast_rows(src2d, nparts):
        a = src2d.ap
        return bass.AP(src2d.tensor, src2d.offset, [[0, nparts], [a[-1][0], a[-1][1]]])

    P.dma("pool", ident[:], dr["c_ident"], writes=["ident"])
    P.dma("pool", bdm[:], dr["c_bd"], writes=["bdm"])
    P.dma("pool", onesm[:], dr["c_ones"], writes=["onesm"])
    for nm in gcols:
        P.dma("sp", gcols[nm][:], dr[nm], writes=[nm])
    P.dma("sp", qgc[0:64, :], dr["qg"], writes=["qgc"])
    P.dma("sp", qgc[64:128, :], dr["qg"], writes=["qgc"])
    P.dma("sp", kgc[0:64, :], dr["kg"], writes=["kgc"])
    P.dma("sp", kgc[64:128, :], dr["kg"], writes=["kgc"])
    P.dma("pool", TW4[:], dr["c_tw4"], writes=["TW4"])
    P.dma("sp", gvrow[:], bcast_rows(dr["gv"], 128), writes=["gvrow"])
    P.dma("sp", b2row[:], bcast_rows(dr["b2"][1:2, :], 128), writes=["b2row"])
    b2c = dr["b2"][0:1, :]
    b2c_ap = bass.AP(b2c.tensor, b2c.offset, [[1, 64], [1, 1]])
    P.dma("sp", b2col[0:64, :], b2c_ap, writes=["b2col"])
    P.dma("sp", b2col[64:128, :], b2c_ap, writes=["b2col"])
    P.dma("pool", w2sb[:], dr["w2"], writes=["w2sb"])
    P.dma("pool", pesb[0:64], dr["pe"], writes=["pesb"])
    P.dma("pool", pesb[64:128], dr["pe"], writes=["pesb"])
    for h in range(8):
        P.dma("sp", bT[64 * (h % 2):64 * (h % 2) + 64, h // 2, :], bcast_rows(dr["bs"][h:h + 1, :], 64), writes=["bT"])
    for g in range(2):
        P.dma("pool", KS[g][64 * (1 - g):64 * (1 - g) + 64, :], dr["c_erows"], writes=[("KS", g)])
    for g in range(2):
        P.dma("pool", VCF[:, g, :, 64:126], dr["c_vcf_static"], writes=["VCF"])
    P.op("dve", lambda e: e.memset(Vs[:, :, :, 64:65], 1.0), writes=["Vs"])
    P.op("dve", lambda e: e.memset(Vw[:, :, :, 64:65], 1.0), writes=["Vw"])
    P.op("dve", lambda e: e.memset(hidT[:], 0.0), writes=["hidT"])
    P.op("dve", lambda e: e.memset(kcT[:], 0.0), writes=["kcT"])
    for g in range(2):
        P.op("dve", lambda e, g=g: e.memset(NS[g][:], 0.0), writes=[("NS", g)])
    for kv in range(2):
        P.op("dve", lambda e, kv=kv: e.memset(csrc[kv][:, 0:16], 0.0), writes=[("csrc", kv)])
    P.dma("sp", Fr[4][:, 0:128], dr["c_tril"], writes=[("Fr", 4)])
    for hh in range(2):
        st = Fr[hh][:].rearrange("p (a t) -> p a t", a=4)
        P.dma("sp", st, dr["wst"][:, 4 * hh:4 * hh + 4, :], writes=[("Fr", hh)])
        TT(WsT[:, 4 * hh:4 * hh + 4, :], st, Fr[4][:, 0:128].unsqueeze(1).to_broadcast([128, 4, 128]), ALU.mult,
           [("Fr", hh), ("Fr", 4)], ["WsT"])
    for kv in range(2):
        P.dma("sp", hbias[:, kv, :], bcast_rows(dr["b1"][kv:kv + 1, :], 128), writes=[("hbias0", kv)])

    scr = {}
    seen = set()

    def wload(dst, name, key, sl, res, wt=True):
        resl = res if isinstance(res, list) else [res]
        if name not in scr:
            scr[name] = nc.dram_tensor("scr_" + name, list(shapes[name]), BF16).ap()
        k = (name, key)
        if k not in seen:
            P.dma("pool", dst, sl(dr[name]), writes=resl)
            if wt:
                seen.add(k)
                P.dma("sp", sl(scr[name]), dst, reads=resl, writes=[("scr", k)])
        else:
            P.dma("sp", dst, sl(scr[name]), reads=[("scr", k)], writes=resl)

    def w1view(i):
        return wA[i][:].rearrange("p a b -> p (a b)").rearrange("p (j c) -> p j c", j=4)

    def load_w1(kv, jq):
        i = ring("wA", 6)
        sl = lambda a: a[kv, :, 4 * jq:4 * jq + 4, :]
        first = ("w1", (kv, jq)) not in seen
        v_ = w1view(i)
        if first:
            wload(v_[64:128, :, :], "w1", (kv, jq), sl, ("wA", i), wt=False)
        wload(v_[0:64, :, :], "w1", (kv, jq), sl, ("wA", i))
        if not first:
            wload(v_[64:128, :, :], "w1", (kv, jq), sl, ("wA", i))
        return i

    def setup_tables():
        P.dma("sp", rbx[0:32, :], dr["rbt"], writes=["rbx"])
        P.op("dve", lambda e: e.memset(rbx[32:33, :], 1.0), reads=[], writes=["rbx1"])
        ohs = ohsb[:, :]
        P.dma("sp", ohs, dr["c_oh"], writes=["ohsb"])
        for h in range(8):
            bk = psb[3 + h % 2]
            MM(bk[:, :], rbx[:, h:h + 1].to_broadcast([33, 128]), ohs, True, True, ["rbx", "rbx1", "ohsb"], [("ps", 3 + h % 2)])
            fi = ring("Frs", 3)
            P.op("act", lambda e, bk=bk, fi=fi: e.copy(out=Fr[fi][:], in_=bk[:, :]), [("ps", 3 + h % 2)], [("Fr", fi)])
            P.dma("sp", fsc.ap()[h], Fr[fi][:], reads=[("Fr", fi)], writes=[("fsc", h)])
            yield
        for g in range(2):
            for (tab, off, step, np_) in ((TD, 256, 511, 128), (TS1, 384, 511, 128), (TCn, 353, 496, 16)):
                fi = ring("Frs", 3)
                for hh in range(4):
                    h = 4 * g + hh
                    P.dma("sp", Fr[fi][0:np_, 128 * hh:128 * hh + 128], bass.AP(fsc, h * 65536 + off, [[step, np_], [1, 128]]),
                          reads=[("fsc", h)], writes=[("Fr", fi)])
                src = Fr[fi][0:np_, :].rearrange("p (h q) -> p h q", h=4)
                TS(tab[0][0:np_, g, :, :], src, 8.0, None, ALU.mult, None, [("Fr", fi)], ["tabs"])
                STT(tab[1][0:np_, g, :, :], src, 8.0, tab[0][0:np_, g, :, :], ALU.mult, ALU.subtract, [("Fr", fi), "tabs"], ["tabs"])
                yield


    def setup_hbias():
        for kv in range(2):
            for jq in range(8):
                i = load_w1(kv, jq)
                for jj in range(4):
                    j = 4 * jq + jj
                    MM(psb[6][:, 0:256], pesb[0:64, kv, j:j + 1].to_broadcast([64, 128]), w1view(i)[0:64, jj, :],
                       j == 0, j == 31, [("wA", i), "pesb"], [("ps", 6)])
                yield
            TT(hbias[:, kv, :], psb[6][:, 0:256], hbias[:, kv, :], ALU.add, [("ps", 6), ("hbias0", kv)], [("hbias", kv)])


    NFR, NPT = 5, 6
    WL, WH, WN = 0.6, 2.0, 7.0
    x1scr = nc.dram_tensor("x1scr", [2, 128, 8, TB], F32).ap()
    G_, U_, D_ = 0, 1, 2
    SB_ = (3, 4)
    OC_, OWS_ = 5, 6

    def norm_fm(gname, dst_is_h):
        for k in range(8):
            ACT(actT[:, k, :], xb[:, k, :], AF.Square, [("xb", k)], [("act", k)])
        for k in range(8):
            MM(psb[D_][:, :], onesm[:, :], actT[:, k, :], k == 0, k == 7, [("act", k), "onesm"], [("ps", D_)])
        ACT(rstdb[:], psb[D_][:, :], AF.Sqrt, [("ps", D_)], ["rstdb"], scale=1.0 / D, bias=EPS)
        DV("reciprocal", ["rstdb"], ["rstdb"], out=rstdb[:], in_=rstdb[:])
        yield WN
        for k in range(8):
            if dst_is_h:
                STT(hT[:, k, :], xb[:, k, :], gcols[gname][:, k:k + 1], rstdb[:], ALU.mult, ALU.mult,
                    [("xb", k), "rstdb", gname], [("hT", k)])
            else:
                STT(xb[:, k, :], xb[:, k, :], gcols[gname][:, k:k + 1], rstdb[:], ALU.mult, ALU.mult,
                    [("xb", k), "rstdb", gname], [("xb", k)])
        yield WN

    def load_wA(name, idx):
        i = ring("wA", 6)
        wload(wA[i][:], name, idx, lambda a: a[idx], ("wA", i))
        return i

    def ffn(tag):
        for f in range(NF):
            ig = load_wA("wg" + tag, f)
            iu = load_wA("wu" + tag, f)
            for k in range(8):
                MM(psb[G_][:, :], wA[ig][:, k, :], hT[:, k, :], k == 0, k == 7, [("wA", ig), ("hT", k)], [("ps", G_)])
            yield WL
            for k in range(8):
                MM(psb[U_][:, :], wA[iu][:, k, :], hT[:, k, :], k == 0, k == 7, [("wA", iu), ("hT", k)], [("ps", U_)])
            fi = ring("Fr", NFR)
            ACT(Fr[fi][:], psb[G_][:, :], AF.Tanh, [("ps", G_)], [("Fr", fi)], scale=0.5)
            STT(Fr[fi][:], Fr[fi][:], 1.0, psb[G_][:, :], ALU.add, ALU.mult, [("Fr", fi), ("ps", G_)], [("Fr", fi)])
            STT(actT[:, f, :], Fr[fi][:], 0.5, psb[U_][:, :], ALU.mult, ALU.mult, [("Fr", fi), ("ps", U_)], [("act", f)])
            yield WL
        for d in range(8):
            i = ring("wD", 2)
            wload(wD[i][:], "wd" + tag, d, lambda a, d=d: a[d], ("wD", i))
            po = (D_, G_, U_)[d % 3]
            for f in range(NF):
                MM(psb[po][:, :], wD[i][:, f, :], actT[:, f, :], f == 0, f == NF - 1, [("wD", i), ("act", f)], [("ps", po)])
                if f % 8 == 7:
                    yield WL
            STT(xb[:, d, :], psb[po][:, :], 0.5, xb[:, d, :], ALU.mult, ALU.add, [("ps", po), ("xb", d)], [("xb", d)])
            yield WL

    def rstd_bd(ps_ap, ps_r, n, scale):
        ACT(sqb[:, 0:n], ps_ap, AF.Square, [ps_r], ["sqb"])
        MM(psb[D_][:, 0:n], bdm[:, :], sqb[:, 0:n], True, True, ["sqb", "bdm"], [("ps", D_)])
        fi = ring("Fr", NFR)
        ACT(Fr[fi][:, 0:n], psb[D_][:, 0:n], AF.Sqrt, [("ps", D_)], [("Fr", fi)], scale=scale, bias=EPS)
        DV("reciprocal", [("Fr", fi)], [("Fr", fi)], out=Fr[fi][:, 0:n], in_=Fr[fi][:, 0:n])
        return fi

    class Pipe:
        def __init__(self):
            self.pend_e = None
            self.q = []
            self.n = 0

        def push(self, lhsT, rhs, m, table, reads, pv):
            sp_ = SB_[self.n % 2]
            self.n += 1
            MM(psb[sp_][0:m, :], lhsT, rhs, True, table is None, reads, [("ps", sp_)])
            if table is not None:
                o3 = psb[sp_][0:m, :].rearrange("p (h q) -> p h q", h=4)
                for ti, tb_ in enumerate(table):
                    MM(o3, ident[0:m, 0:m], tb_, False, ti == len(table) - 1, ["ident", "tabs", "TW4"], [("ps", sp_)])
            self._exp()
            self.pend_e = (sp_, m, pv)

        def _exp(self):
            if self.pend_e is not None:
                sp_, m, pv = self.pend_e
                self.pend_e = None
                pi = ring("PT", NPT)
                ACT(PT[pi][0:m, :], psb[sp_][0:m, :], AF.Exp, [("ps", sp_)], [("PT", pi)], scale=0.125)
                self.q.append((pv, pi))

        def pv(self):
            if len(self.q) > 1:
                pv, pi = self.q.pop(0)
                pv(pi)

        def flush(self):
            self._exp()
            while self.q:
                pv, pi = self.q.pop(0)
                pv(pi)

    pipe = Pipe()

    def stage_A(b):
        t0 = b * TB
        tsl = slice(t0, t0 + TB)
        bb = b % 2
        kwsl = slice(t0 % 2048, t0 % 2048 + TB)
        for k in range(8):
            P.dma("sp", xb[:, k, :], dr["xT"][128 * k:128 * k + 128, tsl], writes=[("xb", k)])
        for g in range(2):
            P.dma("pool", VCN[:, bb, g, :, 64:126], dr["c_vcn_static"][:, 4 * b:4 * b + 4, :], writes=[("VCN", bb)])
        yield from norm_fm("ffn1_norm_g", True)
        yield from ffn("1")
        yield from norm_fm("mix_norm_g", True)
        for k in range(8):
            P.dma("sp", x1scr[bb, :, k, :], xb[:, k, :], reads=[("xb", k)], writes=[("x1scr", bb, k)])
        for j in range(12):
            iw = load_wA("wfm", j)
            pb_ = (G_, U_)[j % 2]
            bank = psb[pb_]
            for k in range(8):
                MM(bank[:, :], wA[iw][:, k, :], hT[:, k, :], k == 0, k == 7, [("wA", iw), ("hT", k)], [("ps", pb_)])
            if j < 4:
                ACT(uT[:, j, :], bank[:, :], AF.Gelu_apprx_tanh, [("ps", pb_)], [("uT", j)])
            elif j < 8:
                i = j - 4
                fi = rstd_bd(bank[:, :], ("ps", pb_), TB, 1.0 / 64)
                for g in range(2):
                    rs = slice(64 * g, 64 * g + 64)
                    STT(QS[rs, bb, g, :, i, :], bank[rs, :].rearrange("p (c q) -> p c q", c=4), qgc[rs, 0:1],
                        Fr[fi][rs, :].rearrange("p (c q) -> p c q", c=4), ALU.mult, ALU.mult,
                        [("ps", pb_), ("Fr", fi), "qgc"], [("QS", bb, g)])
            elif j < 10:
                kv = j - 8
                P.op("act", lambda e, kv=kv, bank=bank: e.copy(out=csrc[kv][:, 16:16 + TB], in_=bank[:, :]),
                     [("ps", pb_)], [("csrc", kv)])
            elif j == 10:
                fi = rstd_bd(bank[:, :], ("ps", pb_), TB, 1.0 / 64)
                for g in range(2):
                    rs = slice(64 * g, 64 * g + 64)
                    STT(KS[g][rs, tsl], bank[rs, :], kgc[rs, 1:2], Fr[fi][rs, :], ALU.mult, ALU.mult,
                        [("ps", pb_), ("Fr", fi), "kgc"], [("KS", g)])
            else:
                fi = rstd_bd(bank[:, :], ("ps", pb_), TB, 1.0 / 64)
                STT(KW[:, kwsl], bank[:, :], kgc[:, 2:3], Fr[fi][:, :], ALU.mult, ALU.mult,
                    [("ps", pb_), ("Fr", fi), "kgc"], ["KW"])
            yield
        wtv = actT[:, 8:16, :].rearrange("p a t -> p (a t)").rearrange("p (k n) -> p k n", k=8)
        wtr = actT[:, 16:21, :].rearrange("p a t -> p (a t)")[:, 0:8 * 280].rearrange("p (k n) -> p k n", k=8)
        wtv_res = [("act", k) for k in range(8, 16)]
        wtr_res = [("act", k) for k in range(16, 21)]
        wload(wtv, "wtm_v", 0, lambda a: a, wtv_res)
        wload(wtr, "wtm_r", 0, lambda a: a, wtr_res)
        for ci in range(4):
            c = 4 * b + ci
            qs = slice(128 * ci, 128 * ci + 128)
            for k in range(8):
                MM(psb[G_][:, :], hT[:, k, qs], wtv[:, k, :], k == 0, k == 7, [("hT", k)] + wtv_res, [("ps", G_)])
            yield 1.0
            for k in range(8):
                MM(psb[U_][:, 0:280], hT[:, k, qs], wtr[:, k, :], k == 0, k == 7, [("hT", k)] + wtr_res, [("ps", U_)])
            fi = ring("Fr", NFR)
            ACT(Fr[fi][:], psb[G_][:, :], AF.Gelu_apprx_tanh, [("ps", G_)], [("Fr", fi)])
            f2 = ring("Fr", NFR)
            ACT(Fr[f2][:], Fr[fi][:], AF.Square, [("Fr", fi)], [("Fr", f2), "ssv"], accum_out=small[:, 0:1])
            ACT(small[:, 1:2], small[:, 0:1], AF.Sqrt, ["ssv"], ["rsv"], scale=1.0 / 512, bias=EPS)
            DV("reciprocal", ["rsv"], ["rsv2"], out=small[:, 2:3], in_=small[:, 1:2])
            STT(vnb[:], Fr[fi][:], small[:, 2:3], gvrow[:], ALU.mult, ALU.mult, [("Fr", fi), "rsv2", "gvrow"], ["vnb"])
            P.op("act", lambda e, c=c: e.copy(out=Vs[:, :, c, 0:64], in_=psb[U_][:, 0:128].rearrange("p (g d) -> p g d", g=2)),
                 [("ps", U_)], ["Vs"])
            P.op("act", lambda e, c=c: e.copy(out=Vw[:, :, c % 16, 0:64], in_=psb[U_][:, 128:256].rearrange("p (g d) -> p g d", g=2)),
                 [("ps", U_)], ["Vw"])
            ACT(gates[:, bb, ci, :], psb[U_][:, 256:280], AF.Tanh, [("ps", U_)], [("gates", bb)], scale=0.5)
            TS(gates[:, bb, ci, :], gates[:, bb, ci, :], 0.5, 0.5, ALU.mult, ALU.add, [("gates", bb)], [("gates", bb)])
            yield WH
            yield WH
            for jj in range(4):
                for hh in range(2):
                    h = 2 * jj + hh
                    MM(psb[D_][64 * hh:64 * hh + 64, 128 * jj:128 * jj + 128], vnb[:, 64 * h:64 * h + 64], WsT[:, h, :],
                       True, True, ["vnb", "WsT"], [("ps", D_)], skip_group_check=True)
            f3 = ring("Fr", NFR)
            TT(Fr[f3][:], psb[D_][:, :], bT[:].rearrange("p a t -> p (a t)"), ALU.add, [("ps", D_), "bT"], [("Fr", f3)])
            TT(yT[:, bb, 0:4, qs], Fr[f3][:].rearrange("p (a t) -> p a t", a=4), uT[:, :, qs], ALU.mult,
               [("Fr", f3)] + [("uT", j) for j in range(4)], [("yT", bb, ci)])
            yield WH
        n0 = 32 * b - 1 if b > 0 else 0
        n1 = 32 * b + 30
        N = n1 - n0 + 1
        base_col = 16 * n0 - (t0 - 16)
        for kv in range(2):
            for jq in range(8):
                i = load_w1(kv, jq)
                for jj in range(4):
                    j = 4 * jq + jj
                    for g in range(2):
                        rs = slice(64 * g, 64 * g + 64)
                        src = csrc[kv][rs, base_col + j: base_col + j + 16 * (N - 1) + 1: 16]
                        MM(psb[(G_, U_)[g]][0:N, 0:256], src, w1view(i)[rs, jj, :], j == 0, j == 31,
                           [("csrc", kv), ("wA", i)], [("ps", (G_, U_)[g])])
                yield 1.5
            for g in range(2):
                pg_ = (G_, U_)[g]
                TT(hid32[0:N, :], psb[pg_][0:N, 0:256], hbias[0:N, kv, :], ALU.add, [("ps", pg_), ("hbias", kv)], ["hid32"])
                ACT(hidb[0:N, :], hid32[0:N, :], AF.Gelu_apprx_tanh, ["hid32"], ["hidb"])
                for ch in range(2):
                    TR(ps7[:, 32 * ch:32 * ch + N], hidb[0:N, 128 * ch:128 * ch + 128], ["hidb"], [("ps", 7)])
                P.op("act", lambda e, kv=kv, g=g, N=N, n0=n0: e.copy(
                    out=hidT[:, kv, g, :, 8 + n0:8 + n0 + N], in_=ps7[:, 0:64].rearrange("p (c n) -> p c n", c=2)[:, :, 0:N]),
                    [("ps", 7)], ["hidT"])
                yield WH
        for kv in range(2):
            P.op("dve", lambda e, kv=kv: e.tensor_copy(out=csrc[kv][:, 0:16], in_=csrc[kv][:, TB:TB + 16]),
                 [("csrc", kv)], [("csrc", kv)])
        for g in range(2):
            for ch in range(2):
                MM(psb[D_][64 * g:64 * g + 64, 0:N], w2sb[:, 0, ch, :], hidT[:, 0, g, ch, 8 + n0:8 + n0 + N],
                   ch == 0, ch == 1, ["w2sb", "hidT"], [("ps", D_)], skip_group_check=True)
        TS(kc32[:, 0:N], psb[D_][:, 0:N], b2col[:, 0:1], None, ALU.add, None, [("ps", D_), "b2col"], ["kc32"])
        ACT(kcsq[:, 0:N], kc32[:, 0:N], AF.Square, ["kc32"], ["kcsq"])
        MM(psb[D_][:, 0:N], bdm[:, :], kcsq[:, 0:N], True, True, ["kcsq", "bdm"], [("ps", D_)])
        ACT(kcr[:, 0:N], psb[D_][:, 0:N], AF.Sqrt, [("ps", D_)], ["kcr"], scale=1.0 / 64, bias=EPS)
        DV("reciprocal", ["kcr"], ["kcr"], out=kcr[:, 0:N], in_=kcr[:, 0:N])
        STT(kcT[:, 8 + n0:8 + n0 + N], kc32[:, 0:N], kgc[:, 0:1], kcr[:, 0:N], ALU.mult, ALU.mult, ["kc32", "kcr", "kgc"], ["kcT"])
        yield WH
        nts = sorted({n0 // 128, n1 // 128})
        for g in range(2):
            for nt in nts:
                for ch in range(2):
                    MM(psb[D_][:, 0:64], hidT[:, 1, g, ch, 8 + 128 * nt:8 + 128 * nt + 128], w2sb[:, 1, ch, :],
                       ch == 0, ch == 1, ["hidT", "w2sb"], [("ps", D_)])
                TT(VCF[:, g, nt, 0:64], psb[D_][:, 0:64], b2row[:, :], ALU.add, [("ps", D_), "b2row"], ["VCF"])
            for ci in range(4):
                c = 4 * b + ci
                for ch in range(2):
                    MM(psb[D_][0:16, 0:64], hidT[:, 1, g, ch, 8 * c:8 * c + 16], w2sb[:, 1, ch, :],
                       ch == 0, ch == 1, ["hidT", "w2sb"], [("ps", D_)])
                TT(VCN[:, bb, g, ci, 0:64], psb[D_][0:16, 0:64], b2row[0:16, :], ALU.add, [("ps", D_), "b2row"], [("VCN", bb)])
                if c == 0:
                    P.op("dve", lambda e, g=g: e.memset(VCN[0:8, 0, g, 0, 0:64], 0.0), [("VCN", 0)], [("VCN", 0)])
            yield WH

    def stage_B(b):
        bb = b % 2
        P.dma("sp", selm[:], dr["c_selmask"][b], writes=["selm"])
        for ci in range(4):
            c = 4 * b + ci
            for g in range(2):
                qr = slice(64 * g, 64 * g + 64)
                sr = slice(64 * (1 - g), 64 * (1 - g) + 64)
                rhs_q = QS[qr, bb, g, ci, :, :].rearrange("p h q -> p (h q)")
                rhs_full = QS[:, bb, g, ci, :, :].rearrange("p h q -> p (h q)")
                gsl = lambda br: gates[:, bb, ci, br * 8 + g * 4: br * 8 + g * 4 + 4]
                QR = ("QS", bb, g)
                oc3 = psb[OC_][:, 0:504].rearrange("p (h x) -> p h x", h=4)

                stC = [True]

                def pv_c(pi, m, nt):
                    for h in range(4):
                        vrhs = VCN[0:16, bb, g, ci, :] if nt is None else VCF[0:m, g, nt, :]
                        MM(psb[OC_][:, 126 * h:126 * h + 126], PT[pi][0:m, 128 * h:128 * h + 128], vrhs,
                           stC[0], False, [("PT", pi), ("VCN", bb), "VCF"], [("ps", OC_)], skip_group_check=True)
                        stC[0] = False

                tiles = [(None, 16, 8 * c, [TCn[0][:, g, :, :], TCn[1][:, g, :, :]])]
                nf = 8 * c - 8
                if nf > 0:
                    tiles.append((0, min(nf, 128), 8, None))
                if nf > 128:
                    tiles.append((1, nf - 128, 8 + 128, None))
                for (nt, m, col, table) in tiles:
                    pipe.push(kcT[qr, col:col + m], rhs_q, m, table, ["kcT", QR],
                              lambda pi, m=m, nt=nt: pv_c(pi, m, nt))
                    yield
                    pipe.pv()

                def make_pv(Vt, kt, ringw, state):
                    def pv(pi):
                        for h in range(4):
                            vrhs = Vt[:, g, kt % 16, :] if ringw else Vt[:, g, kt, :]
                            MM(psb[OWS_][:, 65 * h:65 * h + 65], PT[pi][:, 128 * h:128 * h + 128], vrhs,
                               state[0], False, [("PT", pi), "Vs", "Vw"], [("ps", OWS_)], skip_group_check=True)
                            state[0] = False
                    return pv

                def table_for(kt, br):
                    if kt == c:
                        return [TD[0][:, g, :, :], TD[1][:, g, :, :]]
                    if kt == c - 1:
                        return [TS1[0][:, g, :, :], TS1[1][:, g, :, :]]
                    if br == 2 and kt == c - 4:
                        return [TW4[:, :].unsqueeze(1).to_broadcast([128, 4, 128])]
                    return None

                wk = [kt for kt in range(c - 4, c + 1) if kt >= 0]
                stW = [True]
                chain_it = None
                def chain_gen():
                    TS(small[:, 8:12], oc3[:, :, 64], 1e-30, None, ALU.max, None, [("ps", OC_)], ["rlc"])
                    DV("reciprocal", ["rlc"], ["rlc2"], out=small[:, 12:16], in_=small[:, 8:12])
                    for h in range(4):
                        src = psb[OC_][:, 126 * h + 65:126 * h + 126]
                        if h == 0:
                            STT(impb[:], src, small[:, 12:13], selm[:, ci, :], ALU.mult, ALU.add,
                                [("ps", OC_), "rlc2", "selm"], ["impb"])
                        else:
                            STT(impb[:], src, small[:, 12 + h:13 + h], impb[:], ALU.mult, ALU.add,
                                [("ps", OC_), "rlc2", "impb"], ["impb"])
                        if h == 1:
                            yield
                    TT(small[:, 16:20], gsl(0), small[:, 12:16], ALU.mult, [("gates", bb), "rlc2"], ["coef"])
                    TT(yb[:, 256 * g:256 * g + 256].rearrange("p (h d) -> p h d", h=4), oc3[:, :, 0:64],
                       small[:, 16:20].unsqueeze(2).to_broadcast([128, 4, 64]), ALU.mult,
                       [("ps", OC_), "coef"], [("yb", g)])
                    yield
                    DV("max", ["impb"], ["m8a"], out=m8a[:], in_=impb[:])
                    DV("match_replace", ["m8a", "impb"], ["impw"], out=impw[:], in_to_replace=m8a[:], in_values=impb[:], imm_value=-3e9)
                    yield
                    DV("max", ["impw"], ["m8b"], out=m8b[:], in_=impw[:])
                    if c < 31:
                        TS(NS[g][:, sr.start + 1:sr.start + 62], impb[:], m8b[:, 6:7], -BIG, ALU.is_lt, ALU.mult,
                           ["impb", "m8b"], [("NS", g)])
                    else:
                        TS(NS[g][0:64, sr.start + 1:sr.start + 62], impb[0:64, :], m8b[0:64, 5:6], -BIG, ALU.is_lt, ALU.mult,
                           ["impb", "m8b"], [("NS", g)])
                        TS(NS[g][64:128, sr.start + 1:sr.start + 62], impb[64:128, :], m8b[64:128, 4:5], -BIG, ALU.is_lt, ALU.mult,
                           ["impb", "m8b"], [("NS", g)])

                for wi, kt in enumerate(wk):
                    kc0 = (128 * kt) % 2048
                    pipe.push(KW[qr, kc0:kc0 + 128], rhs_q, 128, table_for(kt, 2), ["KW", QR], make_pv(Vw, kt, True, stW))
                    yield
                    pipe.pv()
                    if wi == 1:
                        chain_it = chain_gen()
                        next(chain_it, None)
                    if chain_it is not None:
                        next(chain_it, None)
                    yield
                if chain_it is None:
                    pipe.flush()
                    chain_it = chain_gen()
                for _ in chain_it:
                    pass
                pipe.flush()
                TR(ps7[:, 256:384], NS[g][:, :], [("NS", g)], [("ps", 7)])
                P.op("dve", lambda e, g=g, ci=ci, sr=sr: e.tensor_copy(
                    out=QS[sr, bb, g, ci, :, :], in_=ps7[sr, 256:384].unsqueeze(1).to_broadcast([64, 4, 128])),
                    [("ps", 7)], [QR])

                def combine(br):
                    o3 = psb[OWS_][:, 0:260].rearrange("p (h x) -> p h x", h=4)
                    DV("reciprocal", [("ps", OWS_)], ["rl"], out=small[:, 20:24], in_=o3[:, :, 64])
                    TT(small[:, 24:28], gsl(br), small[:, 20:24], ALU.mult, [("gates", bb), "rl"], ["coef2"])
                    TT(vtmp[:], o3[:, :, 0:64], small[:, 24:28].unsqueeze(2).to_broadcast([128, 4, 64]), ALU.mult,
                       [("ps", OWS_), "coef2"], ["vtmp"])
                    ysl = yb[:, 256 * g:256 * g + 256]
                    if br == 2:
                        TT(ysl, ysl, vtmp[:].rearrange("p h d -> p (h d)"), ALU.add, ["vtmp", ("yb", g)], [("yb", g)])
                    else:
                        TT(ybb[:, 256 * g:256 * g + 256], ysl, vtmp[:].rearrange("p h d -> p (h d)"), ALU.add,
                           ["vtmp", ("yb", g)], [("ybb", g)])

                combine(2)
                yield
                sk = [kt for kt in (c - 1, c) if kt >= 0] + list(range(0, c - 1))
                stS = [True]
                for kt in sk:
                    ksl = slice(128 * kt, 128 * kt + 128)
                    pipe.push(KS[g][:, ksl], rhs_full, 128, table_for(kt, 1), [("KS", g), QR], make_pv(Vs, kt, False, stS))
                    yield
                    pipe.pv()
                pipe.flush()
                combine(1)
                yield
            for jj in range(4):
                TR(ps7[:, 512 + 128 * jj:512 + 128 * jj + 128], ybb[:, 128 * jj:128 * jj + 128],
                   [("ybb", 0), ("ybb", 1)], [("ps", 7)])
            P.op("dve", lambda e, ci=ci: e.tensor_copy(out=yT[:, bb, 4:8, 128 * ci:128 * ci + 128],
                                                          in_=ps7[:, 512:1024].rearrange("p (a q) -> p a q", a=4)),
                 [("ps", 7)], [("yT", bb, ci)])
            yield

    def stage_C(b):
        t0 = b * TB
        tsl = slice(t0, t0 + TB)
        bb = b % 2
        for k in range(8):
            P.dma("sp", xb[:, k, :], x1scr[bb, :, k, :], reads=[("x1scr", bb, k)], writes=[("xb", k)])
        for d in range(8):
            iw = load_wA("wo", d)
            pb_ = (G_, U_, D_)[d % 3]
            for k in range(8):
                MM(psb[pb_][:, :], wA[iw][:, k, :], yT[:, bb, k, :], k == 0, k == 7,
                   [("wA", iw)] + [("yT", bb, ci) for ci in range(4)], [("ps", pb_)])
            TT(xb[:, d, :], psb[pb_][:, :], xb[:, d, :], ALU.add, [("ps", pb_), ("xb", d)], [("xb", d)])
            yield
        yield from norm_fm("ffn2_norm_g", True)
        yield from ffn("2")
        yield from norm_fm("final_norm_g", False)
        for k in range(8):
            P.dma("sp", outT[128 * k:128 * k + 128, tsl], xb[:, k, :], reads=[("xb", k)], writes=[("outT", k)])
        yield

    def chain(*gens):
        for g_ in gens:
            yield from g_

    def n_att_units(b):
        n = 0
        for c in range(4 * b, 4 * b + 4):
            nC = 1 + (1 if 8 * c - 8 > 0 else 0) + (1 if 8 * c - 8 > 128 else 0)
            nW = min(5, c + 1)
            n += 2 * (nC + 2 * nW + (c + 1) + 2) + 1
        return n

    def interleave(dense, att, n_d, n_a):
        acc = 0.0
        ratio = n_a / max(n_d, 1)
        d_alive = dense is not None
        a_alive = att is not None
        stat = dict(dw=0.0, du=0, au=0, a_after_d=0, d_after_a=0)
        while d_alive or a_alive:
            w = 1.0
            if d_alive:
                try:
                    w = next(dense)
                    w = 1.0 if w is None else w
                    stat["dw"] += w; stat["du"] += 1
                    if not a_alive:
                        stat["d_after_a"] += 1
                except StopIteration:
                    d_alive = False
            if a_alive:
                acc += ratio * w if d_alive else 1.0
                while acc >= 1.0 and a_alive:
                    acc -= 1.0
                    try:
                        next(att)
                        stat["au"] += 1
                        if not d_alive:
                            stat["a_after_d"] += 1
                    except StopIteration:
                        a_alive = False
        if SCHED_DEBUG:
            print("sched: n_d_est=%.1f dense_w=%.1f dense_units=%d  n_a_est=%d att_units=%d  att_after_dense=%d dense_after_att=%d"
                  % (n_d, stat["dw"], stat["du"], n_a, stat["au"], stat["a_after_d"], stat["d_after_a"]))

    def precast_ffn2():
        def one(name, key, sl, npc=8):
            if name not in scr:
                scr[name] = nc.dram_tensor("scr_" + name, list(shapes[name]), BF16).ap()
            i = ring("pcs", 2)
            P.dma("pool", pcs[i][:, 0:npc, :], sl(dr[name]), writes=[("pcs", i)])
            P.dma("pool", sl(scr[name]), pcs[i][:, 0:npc, :], reads=[("pcs", i)], writes=[("scr", (name, key))])
        for d in range(8):
            one("wo", d, lambda a, d=d: a[d])
            seen.add(("wo", d))
            yield
        for f in range(NF):
            for nm in ("wg2", "wu2"):
                one(nm, f, lambda a, f=f: a[f])
                seen.add((nm, f))
                yield
        for d in range(8):
            for (f0, f1) in ((0, 8), (8, 16), (16, NF)):
                one("wd2", d, lambda a, d=d, f0=f0, f1=f1: a[d][:, f0:f1, :], npc=f1 - f0)
                yield
            seen.add(("wd2", d))

    def zip_gens(a, b_):
        alive = [a, b_]
        while alive:
            for g_ in list(alive):
                try:
                    next(g_)
                    yield
                except StopIteration:
                    alive.remove(g_)

    interleave(stage_A(0), chain(setup_hbias(), setup_tables()), 40, 80)
    for b in range(NBLK):
        parts = []
        n_d = 0
        if b >= 1:
            parts.append(stage_C(b - 1))
            n_d += 8 + 4 * WN + 68 * WL + 1
        if b + 1 < NBLK:
            parts.append(stage_A(b + 1))
            n_d += 4 * WN + 68 * WL + 12 + 4 * (1 + 3 * WH) + 24 + 4 * WH + WH + 2 * WH
        if b == 0:
            interleave(chain(*parts), zip_gens(stage_B(0), precast_ffn2()), n_d, n_att_units(0) + 76)
        else:
            interleave(chain(*parts) if parts else None, stage_B(b), n_d, n_att_units(b))
    for _ in stage_C(NBLK - 1):
        pass

    print('SBUF bytes remaining/partition:', nc.sbuf_bytes_remaining)
    P.emit()
    es.close()
    return nc, list(dbg.keys())


_CACHE = {}


def kernel(**inputs):
    inp = {k: np.asarray(v) for k, v in inputs.items()}
    x = inp["x"]
    w = _prep_weights(inp)
    w.update(_static_consts())
    shapes = {k: v.shape for k, v in w.items()}
    shapes["xT"] = (D, T)
    key = "nc"
    if key not in _CACHE:
        _CACHE[key] = build(shapes)
    nc, dbgk = _CACHE[key]
    in_maps = []
    for bi in range(8):
        m = dict(w)
        m["xT"] = np.ascontiguousarray(x[bi].T.astype(np.float32))
        in_maps.append(m)
    res = run_bass_kernel_spmd(nc, in_maps, core_ids=list(range(8)))
    out = np.stack([np.ascontiguousarray(res.results[bi]["outT"].T) for bi in range(8)], 0).astype(np.float32)
    if DEBUG:
        kernel.dbg = {k: res.results[0]["dbg_" + k] for k in dbgk}
    return out
```
